# Optimizing a Trainium2 kernel written in Bass

```python
import math
import jax, jax.numpy as jnp
from jax import lax
import numpy as np

D_MODEL = 1024
BATCH = 16
SEQ = 2048
DEPTH = 2

GRID_W = 64
CTX_LEN = 256
N_MIXERS = 2
MIXER_ATTN = 0
MIXER_POOL = 1
N_HEADS = 8
HEAD_DIM = 64
V_DIM = 2 * HEAD_DIM
QK_WIDTH = N_HEADS * 2 * HEAD_DIM
V_WIDTH = N_HEADS * V_DIM
ROPE_BASE = 10000.0
ROPE_PAIRS_PER_AXIS = HEAD_DIM // 4
POOL_WINDOWS = (2, 4, 8, 16)
N_POOL_GROUPS = len(POOL_WINDOWS)
POOL_GROUP = D_MODEL // N_POOL_GROUPS
D_FF = -(-8 * D_MODEL // (3 * 256)) * 256
Q_BLOCK = 128
ALPHA = (2.0 * DEPTH) ** 0.25
BETA = (8.0 * DEPTH) ** -0.25
LN_EPS = 1e-5
N_MOD = 6
N_ATTN_LAYERS = (DEPTH + 1) // 2
N_POOL_LAYERS = DEPTH // 2

kernel_name = "hybrid_diffattn_pool_dit_backbone"


def layer_norm(x, g, b):
    xf = x.astype(jnp.float32)
    mu = jnp.mean(xf, axis=-1, keepdims=True)
    var = jnp.mean(jnp.square(xf - mu), axis=-1, keepdims=True)
    return ((xf - mu) * lax.rsqrt(var + LN_EPS) * g + b).astype(x.dtype)


def adaln_params(cond, w_mod, b_mod):
    m = jax.nn.silu(cond) @ w_mod + b_mod
    return jnp.split(m, N_MOD, axis=-1)


def modulate(h, shift, scale):
    return h * (1 + scale) + shift


def axial_rope_tables(rows, dtype):
    row = jnp.repeat(jnp.arange(rows, dtype=jnp.float32), GRID_W)
    col = jnp.tile(jnp.arange(GRID_W, dtype=jnp.float32), rows)
    inv = jnp.power(ROPE_BASE, -jnp.arange(ROPE_PAIRS_PER_AXIS, dtype=jnp.float32) / ROPE_PAIRS_PER_AXIS)
    ang = jnp.concatenate([row[:, None] * inv, col[:, None] * inv], axis=-1)
    cos = jnp.cos(ang)[:, None, None, :].astype(dtype)
    sin = jnp.sin(ang)[:, None, None, :].astype(dtype)
    return cos, sin


def apply_rope(x, cos, sin):
    half = HEAD_DIM // 2
    x1, x2 = x[..., :half], x[..., half:]
    return jnp.concatenate([x1 * cos - x2 * sin, x1 * sin + x2 * cos], axis=-1)


def project_q(h, w_qkv):
    B, S = h.shape[:2]
    return (h @ w_qkv[:, :QK_WIDTH]).reshape(B, S, N_HEADS, 2, HEAD_DIM)


def project_kv(h, w_qkv):
    B, S = h.shape[:2]
    kv = h @ w_qkv[:, QK_WIDTH:]
    k = kv[..., :QK_WIDTH].reshape(B, S, N_HEADS, 2, HEAD_DIM)
    v = kv[..., QK_WIDTH:].reshape(B, S, N_HEADS, V_DIM)
    return k, v


def diff_softmax_mix(q, k, v, lam):
    s = jnp.einsum('bqhmd,bkhmd->bhmqk', q * (HEAD_DIM ** -0.5), k).astype(jnp.float32)
    p = jax.nn.softmax(s, axis=-1)
    a = p[:, :, 0] - lam * p[:, :, 1]
    return jnp.einsum('bhqk,bkhe->bqhe', a.astype(v.dtype), v)


def diff_head_out(o, subln_g, lam_init, w_o):
    B, Q = o.shape[:2]
    of = o.astype(jnp.float32)
    of = of * lax.rsqrt(jnp.mean(of * of, axis=-1, keepdims=True) + LN_EPS) * subln_g * (1.0 - lam_init)
    return of.astype(o.dtype).reshape(B, Q, V_WIDTH) @ w_o


def latent_diff_attention(q_l, k_all, v_all, lam):
    B, S = q_l.shape[:2]
    n_blk = S // Q_BLOCK
    qb = jnp.moveaxis(q_l.reshape(B, n_blk, Q_BLOCK, N_HEADS, 2, HEAD_DIM), 1, 0)
    ob = lax.map(lambda q: diff_softmax_mix(q, k_all, v_all, lam), qb)
    return jnp.moveaxis(ob, 0, 1).reshape(B, S, N_HEADS, V_DIM)


def multiscale_pool(h, w_pool, pool_scale):
    B, S, D = h.shape
    hf = h.astype(jnp.float32)
    cs = jnp.concatenate([jnp.zeros((B, 1, D), jnp.float32), lax.cumsum(hf, axis=1)], axis=1)
    t = jnp.arange(S)
    outs = []
    for g, w in enumerate(POOL_WINDOWS):
        lo = jnp.clip(t - w // 2, 0, S)
        hi = jnp.clip(t + w - w // 2, 0, S)
        cg = cs[..., g * POOL_GROUP:(g + 1) * POOL_GROUP]
        mean = (cg[:, hi] - cg[:, lo]) / (hi - lo).astype(jnp.float32)[None, :, None]
        outs.append(mean - hf[..., g * POOL_GROUP:(g + 1) * POOL_GROUP])
    pooled = jnp.stack(outs, axis=2).astype(h.dtype)
    y = jnp.einsum('bsgi,gio->bsgo', pooled, w_pool).reshape(B, S, D)
    return y * pool_scale


def swiglu(h, w_in, w_out):
    g, u = jnp.split(h @ w_in, 2, axis=-1)
    return (jax.nn.silu(g) * u) @ w_out


def reads_context(layer):
    return (layer % N_MIXERS) == MIXER_ATTN


def setup_inputs(seed: int = 0) -> dict:
    key = jax.random.key(seed)
    ks = jax.random.split(key, 16)
    D = D_MODEL
    f32 = jnp.float32
    return {
        "x": jax.random.normal(ks[0], (BATCH, SEQ, D), f32),
        "c": jax.random.normal(ks[1], (BATCH, D), f32),
        "ctx": jax.random.normal(ks[2], (BATCH, CTX_LEN, D), f32),
        "c_ctx": jax.random.normal(ks[3], (D,), f32),
        "w_mod": jax.random.normal(ks[4], (DEPTH, D, N_MOD * D), f32) * (0.5 * D ** -0.5),
        "b_mod": jax.random.normal(ks[5], (DEPTH, N_MOD * D), f32) * 0.01,
        "ln_g": 1.0 + 0.02 * jax.random.normal(ks[6], (DEPTH, 2, D), f32),
        "ln_b": 0.02 * jax.random.normal(ks[7], (DEPTH, 2, D), f32),
        "attn_w_qkv": jax.random.normal(ks[8], (N_ATTN_LAYERS, D, QK_WIDTH + QK_WIDTH + V_WIDTH), f32) * D ** -0.5,
        "attn_lambda": 0.1 * jax.random.normal(ks[9], (N_ATTN_LAYERS, 4, HEAD_DIM), f32),
        "attn_subln_g": 1.0 + 0.02 * jax.random.normal(ks[10], (N_ATTN_LAYERS, V_DIM), f32),
        "attn_w_o": jax.random.normal(ks[11], (N_ATTN_LAYERS, V_WIDTH, D), f32) * (V_WIDTH ** -0.5 * BETA),
        "pool_w": jax.random.normal(ks[12], (N_POOL_LAYERS, N_POOL_GROUPS, POOL_GROUP, POOL_GROUP), f32) * (POOL_GROUP ** -0.5 * BETA),
        "pool_scale": 0.5 + 0.05 * jax.random.normal(ks[13], (N_POOL_LAYERS, D), f32),
        "ffn_w_in": jax.random.normal(ks[14], (DEPTH, D, 2 * D_FF), f32) * D ** -0.5,
        "ffn_w_out": jax.random.normal(ks[15], (DEPTH, D_FF, D), f32) * (D_FF ** -0.5 * BETA),
    }


def reference(x, c, ctx, c_ctx, w_mod, b_mod, ln_g, ln_b, attn_w_qkv, attn_lambda, attn_subln_g,
              attn_w_o, pool_w, pool_scale, ffn_w_in, ffn_w_out):
    n_lat = x.shape[1]
    rows = n_lat // GRID_W
    cos, sin = axial_rope_tables(rows, x.dtype)
    h_ctx = ctx
    for i in range(DEPTH):
        mixer = i % N_MIXERS
        ctx_later = any(reads_context(j) for j in range(i + 1, DEPTH))
        sh1, sc1, g1, sh2, sc2, g2 = adaln_params(c[:, None, :], w_mod[i], b_mod[i])
        csh1, csc1, cg1, csh2, csc2, cg2 = adaln_params(c_ctx, w_mod[i], b_mod[i])
        hx = modulate(x, sh1, sc1)
        hc = modulate(h_ctx, csh1, csc1)
        if mixer == MIXER_ATTN:
            a = i // N_MIXERS
            lam_init = 0.8 - 0.6 * math.exp(-0.3 * i)
            lp = attn_lambda[a].astype(jnp.float32)
            lam = jnp.exp(jnp.sum(lp[0] * lp[1])) - jnp.exp(jnp.sum(lp[2] * lp[3])) + lam_init
            w_qkv = attn_w_qkv[a]
            k_c, v_c = project_kv(hc, w_qkv)
            q_l = apply_rope(project_q(hx, w_qkv), cos, sin)
            k_l, v_l = project_kv(hx, w_qkv)
            k_l = apply_rope(k_l, cos, sin)
            k_all = jnp.concatenate([k_c, k_l], axis=1)
            v_all = jnp.concatenate([v_c, v_l], axis=1)
            y_lat = diff_head_out(latent_diff_attention(q_l, k_all, v_all, lam),
                                  attn_subln_g[a], lam_init, attn_w_o[a])
            if ctx_later:
                q_c = project_q(hc, w_qkv)
                y_ctx = diff_head_out(diff_softmax_mix(q_c, k_c, v_c, lam),
                                      attn_subln_g[a], lam_init, attn_w_o[a])
        else:
            p = i // N_MIXERS
            y_lat = multiscale_pool(hx, pool_w[p], pool_scale[p])
            if ctx_later:
                y_ctx = multiscale_pool(hc, pool_w[p], pool_scale[p])
        x = layer_norm(ALPHA * x + g1 * y_lat, ln_g[i, 0], ln_b[i, 0])
        x = layer_norm(ALPHA * x + g2 * swiglu(modulate(x, sh2, sc2), ffn_w_in[i], ffn_w_out[i]),
                       ln_g[i, 1], ln_b[i, 1])
        if ctx_later:
            h_ctx = layer_norm(ALPHA * h_ctx + cg1 * y_ctx, ln_g[i, 0], ln_b[i, 0])
            h_ctx = layer_norm(ALPHA * h_ctx + cg2 * swiglu(modulate(h_ctx, csh2, csc2), ffn_w_in[i], ffn_w_out[i]),
                               ln_g[i, 1], ln_b[i, 1])
    return x
```

```python
import math
import contextlib
import numpy as np
import concourse.bass as bass
import concourse.mybir as mybir
from concourse.bass_utils import run_bass_kernel_spmd

F32 = mybir.dt.float32
BF16 = mybir.dt.bfloat16
AF = mybir.ActivationFunctionType
ALU = mybir.AluOpType
AX = mybir.AxisListType
PE, ACT, DVE, POOL, SP = "tensor", "scalar", "vector", "gpsimd", "sync"
ENGS = (PE, ACT, DVE, POOL, SP)

D = 1024
NCH = 8
H = 8
ALPHA = (2.0 * 2) ** 0.25
LN_EPS = 1e-5
WINDOWS = (2, 4, 8, 16)
GRID_W = 64


class Rec:
    __slots__ = ("eng", "fn", "deps", "sig", "count", "dma", "semkey", "cum", "idx")


class Prog:
    def __init__(self, nc):
        self.nc = nc
        self.recs = {e: [] for e in ENGS}
        self.last_w = {}
        self.readers = {}
        self.dma_cum = {}
        self.n = 0
        self.pending = {}
        self.seen = {}

    def switch(self, region):
        lat = {}
        dmas = []
        old = self.pending.get(region, [])
        cand = list(old)
        for k in list(self.last_w.keys()):
            if isinstance(k, tuple) and k[0] == region:
                cand.append(self.last_w.pop(k))
        for k in list(self.readers.keys()):
            if isinstance(k, tuple) and k[0] == region:
                cand.extend(self.readers.pop(k))
        for r in cand:
            if r.dma:
                dmas.append(r)
            else:
                if r.eng not in lat or lat[r.eng].idx < r.idx:
                    lat[r.eng] = r
        self.pending[region] = list(lat.values()) + dmas
        self.seen[region] = set()

    def op(self, eng, fn, reads=(), writes=(), dma=False, semkey=None):
        r = Rec()
        r.eng, r.fn, r.dma, r.sig, r.count, r.idx = eng, fn, dma, False, 0, self.n
        self.n += 1
        deps = set()
        for k in tuple(reads) + tuple(writes):
            if isinstance(k, tuple) and k[0] in self.pending and k not in self.seen[k[0]]:
                self.seen[k[0]].add(k)
                deps.update(self.pending[k[0]])
        for k in reads:
            w = self.last_w.get(k)
            if w is not None:
                deps.add(w)
        for k in writes:
            w = self.last_w.get(k)
            if w is not None:
                deps.add(w)
            deps.update(self.readers.get(k, ()))
        for k in reads:
            self.readers.setdefault(k, []).append(r)
        for k in writes:
            self.last_w[k] = r
            self.readers[k] = []
        deps.discard(r)
        r.deps = deps
        if dma:
            r.semkey = semkey if semkey is not None else writes[0]
            self.dma_cum[r.semkey] = self.dma_cum.get(r.semkey, 0) + 16
            r.cum = self.dma_cum[r.semkey]
        else:
            r.semkey, r.cum = None, 0
        self.recs[eng].append(r)
        return r

    def emit(self, final_waits=()):
        nc = self.nc
        for e in ENGS:
            for r in self.recs[e]:
                for d in r.deps:
                    if d.dma:
                        continue
                    if d.eng == PE and r.eng == PE and not r.dma:
                        continue
                    d.sig = True
        for r in final_waits:
            if not r.dma:
                r.sig = True
        for e in ENGS:
            c = 0
            for r in self.recs[e]:
                if r.dma:
                    continue
                if r.sig:
                    c += 1
                    r.count = c
        with contextlib.ExitStack() as st:
            esem = {e: st.enter_context(nc.semaphore("p_" + e)) for e in ENGS}
            dsem = {}
            for i, k in enumerate(self.dma_cum):
                dsem[k] = st.enter_context(nc.semaphore("d%d" % i))
            block = st.enter_context(nc.Block())

            def make(e):
                def body(eng):
                    waited = {}

                    def wait(sem, val, key):
                        if waited.get(key, 0) >= val:
                            return
                        waited[key] = val
                        eng.wait_ge(sem, val)

                    for r in self.recs[e]:
                        tg = {}
                        for d in r.deps:
                            if d.dma:
                                k_ = ("d", d.semkey)
                                if tg.get(k_, (None, 0))[1] < d.cum:
                                    tg[k_] = (dsem[d.semkey], d.cum)
                            else:
                                if d.eng == PE and e == PE and not r.dma:
                                    continue
                                k_ = ("e", d.eng)
                                if tg.get(k_, (None, 0))[1] < d.count:
                                    tg[k_] = (esem[d.eng], d.count)
                        for k_ in sorted(tg, key=str):
                            wait(tg[k_][0], tg[k_][1], k_)
                        ins = r.fn(eng)
                        if r.dma:
                            ins.then_inc(dsem[r.semkey], 16)
                        elif r.sig:
                            ins.then_inc(esem[e], 1)
                    if e == SP:
                        tg = {}
                        for r in final_waits:
                            if r.dma:
                                k_ = ("d", r.semkey)
                                if tg.get(k_, (None, 0))[1] < r.cum:
                                    tg[k_] = (dsem[r.semkey], r.cum)
                            else:
                                k_ = ("e", r.eng)
                                if tg.get(k_, (None, 0))[1] < r.count:
                                    tg[k_] = (esem[r.eng], r.count)
                        for k_ in sorted(tg, key=str):
                            wait(tg[k_][0], tg[k_][1], k_)
                return body

            for e in ENGS:
                getattr(block, e)(make(e))


def rope_tables(S):
    rows = S // GRID_W
    t = np.arange(S)
    row = (t // GRID_W).astype(np.float32)
    col = (t % GRID_W).astype(np.float32)
    inv = np.power(np.float32(10000.0), -np.arange(16, dtype=np.float32) / np.float32(16)).astype(np.float32)
    ang = np.concatenate([row[:, None] * inv, col[:, None] * inv], axis=-1).astype(np.float32)
    cos = np.cos(ang).astype(np.float32).T
    sin = np.sin(ang).astype(np.float32).T
    cos128 = np.tile(cos, (4, 1))
    sin128 = np.concatenate([-sin, sin, -sin, sin], axis=0)
    return np.ascontiguousarray(cos128), np.ascontiguousarray(sin128)


def pool_tables(S):
    NT = S // 128
    band = np.zeros((4, 5, 128, 128), np.float32)
    invedge = np.zeros((4, 2, 128), np.float32)

    def full(w):
        t = np.arange(S)
        lo = np.clip(t - w // 2, 0, S)
        hi = np.clip(t + w - w // 2, 0, S)
        return lo, hi

    for g, w in enumerate(WINDOWS):
        lo, hi = full(w)
        cnt = (hi - lo)

        def blockmat(ti, si):
            m = np.zeros((128, 128), np.float32)
            for tt in range(128):
                t = ti * 128 + tt
                for tp in range(lo[t], hi[t]):
                    if si * 128 <= tp < (si + 1) * 128:
                        m[tp - si * 128, tt] += 1.0
                if si == ti:
                    m[tt, tt] -= cnt[t]
            return m
        mid = 1 if NT > 2 else 0
        band[g, 0] = blockmat(mid, mid - 1) if mid >= 1 else 0
        band[g, 1] = blockmat(mid, mid) if NT > 2 else 0
        band[g, 2] = blockmat(mid, mid + 1) if mid + 1 < NT else 0
        band[g, 3] = blockmat(0, 0)
        band[g, 4] = blockmat(NT - 1, NT - 1)
        if NT <= 2:
            band[g, 0] = blockmat(1, 0)
            band[g, 2] = blockmat(0, 1)
        invedge[g, 0] = 1.0 / cnt[0:128]
        invedge[g, 1] = 1.0 / cnt[S - 128:S]
    return band, invedge


def build(S=2048, CTX=256, DFF=2816):
    NT = S // 128
    NCT = CTX // 128
    KT = NCT + NT
    QB = min(512, S)
    NQB = S // QB
    TPB = QB // 128
    NFC = DFF // 128
    GC = 4
    groups = [list(range(a, min(a + GC, NFC))) for a in range(0, NFC, GC)]
    NACC = 2 * TPB
    lam_inits = [0.8 - 0.6 * math.exp(-0.3 * 0)]

    nc = bass.Bass("TRN2", target_bir_lowering=False)
    dr = lambda n, s, k="ExternalInput": nc.dram_tensor(n, s, F32, kind=k).ap()
    x_d = dr("x", [2, S, D]); cv_d = dr("cvec", [3, D]); ctx_d = dr("ctx", [2, CTX, D])
    wmod_d = dr("w_mod", [2, D, 6 * D]); bmod_d = dr("b_mod", [2, 6 * D])
    lng_d = dr("ln_g", [2, 2, D]); lnb_d = dr("ln_b", [2, 2, D])
    wqkv_d = dr("w_qkv", [D, 3 * D]); lam_d = dr("lam", [4, 64]); subg_d = dr("subg", [1, 128])
    wo_d = dr("w_o", [D, D]); poolw_d = dr("pool_w", [4, 256, 256]); pscale_d = dr("pool_scale", [1, D])
    win_d = dr("w_in", [2, D, 2 * DFF]); wout_d = dr("w_out", [2, DFF, D])
    cos_d = dr("rope_cos", [128, S]); sin_d = dr("rope_sin", [128, S])
    band_d = dr("band", [4, 5, 128, 128]); inve_d = dr("invedge", [4, 2, 128])
    out_d = dr("out", [2, S, D], "ExternalOutput")
    gd_d = dr("gscratch", [2, 2, 2, D], "Internal")

    st = contextlib.ExitStack()
    with st:
        sb = lambda n, s, d: st.enter_context(nc.sbuf_tensor(n, s, d))
        XR = sb("XR", [128, 16 * 1024], F32)
        hT = sb("hT", [128, NCH, S], BF16)
        R2 = sb("R2", [128, 17 * 1024], F32)
        TG = sb("TG", [128, D], F32); TLG = sb("TLG", [128, D], F32)
        TLB = sb("TLB", [128, D], F32)
        ident = sb("ident", [128, 128], F32)
        identb = sb("identb", [128, 128], BF16)
        permb = sb("permb", [128, 128], BF16)
        mcol = sb("mcol", [128, 2, 48, 4], F32)
        sc2a = sb("sc2a", [128, 2, 8, 4], F32)
        scT = sb("scT", [128, 9, 4], BF16)
        small = sb("small", [128, 64], F32)
        stats = sb("stats", [128, 3, 2, 6], F32)
        mv = sb("mv", [128, 3, 4], F32)
        gsub = sb("gsub", [128, 128], F32)
        ework = sb("ework", [128, 3, 4 * 128], F32)
        accS = sb("accS", [128, 3, 390], F32)
        onb = sb("onb", [128, 4, 128], BF16)
        zt = sb("zt", [128, 3, D], F32)
        negh = sb("negh", [128, 8], F32)
        RT = sb("RT", [128, 2, QB], F32)
        ps = st.enter_context(nc.psum_tensor("ps", [128, 8, 512], F32))
        P = Prog(nc)

        XRv = XR[:].rearrange("p (t d) -> p t d", d=D) if NT == 16 else XR[:, 0:NT * D].rearrange("p (t d) -> p t d", d=D)
        XRb = XR[:].bitcast(BF16)
        XRf = XR[:]
        off = [0]

        def carve_b(n):
            a = XRb[:, off[0]:off[0] + n]
            off[0] += n
            return a
        QT = carve_b(S)
        KA = carve_b(KT * 128); KBm = carve_b(KT * 128)
        VX = carve_b(KT * 136).rearrange("p (k e) -> p k e", e=136)
        WH2 = carve_b(2 * 3 * 1024).rearrange("p (s w k n) -> p s w k n", s=2, w=3, k=8)
        QBF = carve_b(2 * QB).rearrange("p (s q) -> p s q", s=2)
        QF = carve_b(4 * QB).bitcast(F32).rearrange("p (s q) -> p s q", s=2)
        EB = carve_b(2 * 2 * QB).rearrange("p (s c q) -> p s c q", s=2, c=2)
        assert off[0] % 2 == 0
        foff = off[0] // 2
        COS = XRf[:, foff:foff + S]; SIN = XRf[:, foff + S:foff + 2 * S]
        foff += 2 * S
        assert foff <= 16 * 1024, foff

        R2b = R2[:].bitcast(BF16)
        R2f = R2[:]
        onT = R2b[:, 0:NCH * S].rearrange("p (h s) -> p h s", h=NCH)
        WO = R2b[:, 16384:16384 + 8 * D].rearrange("p (k n) -> p k n", k=8)
        hcT = R2b[:, 24576:24576 + NCH * CTX].rearrange("p (c s) -> p c s", c=NCH)
        LAMT = R2f[:, 14336:14336 + 256].rearrange("p (a b) -> p a b", a=4)
        CVT = R2f[:, 0:1024]
        WM = R2b[:, 2048:2048 + 2 * 9 * 512].rearrange("p (s k n) -> p s k n", s=2, k=9)
        GROW = R2f[:, 8192:8192 + 2 * 512].rearrange("p (s n) -> p s n", s=2)
        SCB = R2b[:, 18432:18432 + 9 * 2 * 128].rearrange("p (k n m) -> p k n m", k=9, n=2)
        WIN = R2b[:, 0:2 * 2 * 8 * GC * 128].rearrange("p (s u k n) -> p s u k n", s=2, u=2, k=8)
        o2 = 2 * 2 * 8 * GC * 128
        WOUT = R2b[:, o2:o2 + 2 * GC * D].rearrange("p (s c n) -> p s c n", s=2, c=GC)
        o2 += 2 * GC * D
        AT = R2b[:, o2:o2 + 2 * GC * QB].rearrange("p (s c q) -> p s c q", s=2, c=GC)
        o2 += 2 * GC * QB
        assert o2 % 2 == 0
        f2 = o2 // 2
        WST = R2f[:, f2:f2 + 2 * D].rearrange("p (s n) -> p s n", s=2)
        f2 += 2 * D
        SG = R2f[:, f2:f2 + 2 * QB].rearrange("p (s q) -> p s q", s=2)
        f2 += 2 * QB
        assert f2 <= 17 * 1024, f2
        XH = R2b[:, 0:4 * 2 * D].rearrange("p (s a d) -> p s a d", s=4, a=2)
        PW = R2b[:, 8192:8192 + 4 * 2 * 256].rearrange("p (g c n) -> p g c n", g=4, c=2)
        BAND = R2b[:, 10240:10240 + 20 * 128].rearrange("p (g v n) -> p g v n", g=4, v=5)
        INVE = R2f[:, 6400:6400 + 8 * 128].rearrange("p (g e n) -> p g e n", g=4, e=2)
        PWF = R2f[:, 7424:7424 + 4 * 2 * 256].rearrange("p (g c n) -> p g c n", g=4, c=2)

        def MM(out, lhsT, rhs, start, stop, r, w, skip=False):
            return P.op(PE, lambda e: e.matmul(out, lhsT=lhsT, rhs=rhs, start=start, stop=stop, skip_group_check=skip), r, w)

        def TR(out, in_, idn, r, w):
            return P.op(PE, lambda e: e.transpose(out, in_, idn), r, w)

        def ACTF(out, in_, func, r, w, scale=1.0, bias=0.0):
            return P.op(ACT, lambda e: e.activation(out=out, in_=in_, func=func, scale=scale, bias=bias), r, w)

        def TT(eng, out, a, b_, op, r, w):
            return P.op(eng, lambda e: e.tensor_tensor(out=out, in0=a, in1=b_, op=op), r, w)

        def TS(eng, out, a, s1, op0, r, w, s2=None, op1=None):
            if op1 is None:
                return P.op(eng, lambda e: e.tensor_scalar(out=out, in0=a, scalar1=s1, scalar2=None, op0=op0), r, w)
            return P.op(eng, lambda e: e.tensor_scalar(out=out, in0=a, scalar1=s1, scalar2=s2, op0=op0, op1=op1), r, w)

        def STT(out, in0, scalar, in1, op0, op1, r, w):
            return P.op(DVE, lambda e: e.scalar_tensor_tensor(out=out, in0=in0, scalar=scalar, in1=in1, op0=op0, op1=op1), r, w)

        def CP(eng, out, in_, r, w):
            return P.op(eng, lambda e: e.tensor_copy(out=out, in_=in_), r, w)

        def MS(eng, ap, val, w):
            return P.op(eng, lambda e: e.memset(ap, val), (), w)

        def DMA(eng, out, in_, r, w, semkey=None):
            return P.op(eng, lambda e: e.dma_start(out=out, in_=in_), r, w, dma=True, semkey=semkey)

        bank_rr = [0]

        def PB(i):
            return ps[:, i, :]

        MS(POOL, ident[:], 0.0, ["ident"])
        P.op(POOL, lambda e: e.affine_select(out=ident[:], in_=ident[:], pattern=[[-1, 128]], compare_op=ALU.not_equal,
                                             fill=1.0, base=0, channel_multiplier=1), ["ident"], ["ident"])
        CP(DVE, identb[:], ident[:], ["ident"], ["identb"])
        for f_ in range(2):
            CP(DVE, permb[:].rearrange("p (c f i) -> p c f i", c=2, f=2)[:, :, f_, :],
               ident[:].rearrange("p (c f i) -> p c f i", c=2, f=2)[:, :, 1 - f_, :], ["ident"], [("permb", f_)])
        MS(POOL, negh[:], -0.5, ["negh"])
        MS(POOL, CVT, 0.0, [("R2", "cvt")])
        DMA(SP, CVT[0:3, :], cv_d, (), [("R2", "cvt")])
        for kc in range(8):
            TR(ps[:, kc // 4, (kc % 4) * 128:(kc % 4 + 1) * 128], CVT[:, kc * 128:(kc + 1) * 128], ident[:],
               [("R2", "cvt"), "ident"], [("ps", kc // 4)])
        MS(POOL, scT[:], 1.0, ["scT"])
        for hb in range(2):
            ACTF(scT[:, hb * 4:(hb + 1) * 4, 0:3], ps[:, hb, :].rearrange("p (k n) -> p k n", k=4)[:, :, 0:3], AF.Silu,
                 [("ps", hb), "scT"], ["scT"])
        for n in range(2):
            CP(DVE, SCB[:, :, n, :], scT[:, :, n:n + 1].broadcast_to([128, 9, 128]), ["scT"], [("R2", "SCB", n)])
        for s_ in range(2):
            MS(POOL, WM[:, s_, 8, :], 0.0, [("R2", "wm", s_)])
        fine = lambda l_, cb_: ("mcol", l_, cb_)
        allfine = [fine(l_, cb_) for l_ in range(2) for cb_ in range(12)]
        MS(POOL, mcol[:], 0.0, ["mcol"] + allfine)
        blkc = [0]

        def mod_block(l, cb):
            s_ = blkc[0] % 2
            blkc[0] += 1
            blk = blkc[0]
            DMA(POOL, WM[:, s_, 0:8, :], wmod_d[l].rearrange("(k p) n -> p k n", p=128)[:, :, cb * 512:(cb + 1) * 512],
                (), [("R2", "wm", s_)])
            DMA(POOL, WM[0:1, s_, 8, :], bmod_d[l:l + 1, cb * 512:(cb + 1) * 512], (), [("R2", "wm", s_)],
                semkey=("R2", "wmb", s_))
            need_col = cb not in (4, 5, 10, 11)
            if need_col:
                bk = 6 + (blk % 2)
                for jj in range(4):
                    for kc in range(9):
                        MM(ps[:, bk, jj * 4:jj * 4 + 3], WM[:, s_, kc, jj * 128:(jj + 1) * 128], scT[:, kc, 0:3],
                           kc == 0, kc == 8, [("R2", "wm", s_), "scT"], [("ps", bk)])
                CP(DVE, mcol[:, l, cb * 4:cb * 4 + 4, 0:3], ps[:, bk, 0:16].rearrange("p (j n) -> p j n", j=4)[:, :, 0:3],
                   [("ps", bk), fine(l, cb)], [fine(l, cb)])
            else:
                which = 0 if cb in (4, 5) else 1
                half = cb % 2
                for n in range(2):
                    bk = 4 + n
                    for kc in range(9):
                        MM(PB(bk), SCB[:, kc, n, :], WM[:, s_, kc, :], kc == 0, kc == 8,
                           [("R2", "wm", s_), ("R2", "SCB", n)], [("ps", bk)])
                    ACTF(GROW[:, n, :], PB(bk), AF.Identity, [("ps", bk)], [("R2", "grow", n)])
                    DMA(SP, gd_d[l, n, which:which + 1, half * 512:(half + 1) * 512], GROW[0:1, n, :],
                        [("R2", "grow", n)], [("gd", l, n, which, half)], semkey=("gdst", n))

        tcount_ = [0]

        def ht_tile(b, kind, i, mkeys):
            slot = tcount_[0] % 2
            tcount_[0] += 1
            srcd = ctx_d[b, i * 128:(i + 1) * 128, :] if kind == "c" else x_d[b, i * 128:(i + 1) * 128, :]
            DMA(SP, zt[:, slot, :], srcd, (), [("zt", slot)])
            for j in range(NCH):
                bk = 2 * slot + j // 4
                TR(ps[:, bk, (j % 4) * 128:(j % 4 + 1) * 128], zt[:, slot, j * 128:(j + 1) * 128], ident[:],
                   [("zt", slot), "ident"], [("ps", bk)])
            for j in range(NCH):
                bk = 2 * slot + j // 4
                if kind == "c":
                    dst_, dk_, n_ = hcT[:, j, i * 128:(i + 1) * 128], ("R2", "hcT", j), 2
                else:
                    dst_, dk_, n_ = hT[:, j, i * 128:(i + 1) * 128], ("hT", j, i // TPB), b
                src_ = ps[:, bk, (j % 4) * 128:(j % 4 + 1) * 128]
                if j < 4:
                    ACTF(dst_, src_, AF.Identity, [("ps", bk)] + mkeys, [dk_], scale=mcol[:, 0, 8 + j, n_:n_ + 1], bias=mcol[:, 0, j, n_:n_ + 1])
                else:
                    TS(DVE, dst_, src_, mcol[:, 0, 8 + j, n_:n_ + 1], ALU.mult, [("ps", bk)] + mkeys, [dk_],
                       s2=mcol[:, 0, j, n_:n_ + 1], op1=ALU.add)

        order = [(l_, cb_) for l_ in range(2) for cb_ in range(12) if not (l_ == 1 and cb_ in (0, 1))]
        for (l_, cb_) in order[:4]:
            mod_block(l_, cb_)
        TS(DVE, mcol[:, 0, 8:16, :], mcol[:, 0, 8:16, :], 1.0, ALU.add, [fine(0, 2), fine(0, 3)], [fine(0, 2), fine(0, 3)])
        rest = order[4:]
        mk0 = [fine(0, c_) for c_ in range(4)]
        for i in range(NT):
            ht_tile(0, "x", i, mk0)
            nblk = (len(rest) * (i + 1)) // NT - (len(rest) * i) // NT
            for _ in range(nblk):
                mod_block(*rest.pop(0))
        while rest:
            mod_block(*rest.pop(0))
        TS(DVE, mcol[:, 1, 8:16, :], mcol[:, 1, 8:16, :], 1.0, ALU.add, allfine + ["mcol"], ["mcol"] + allfine)
        for l in range(2):
            TS(DVE, mcol[:, l, 32:40, :], mcol[:, l, 32:40, :], 1.0, ALU.add, ["mcol"], ["mcol"])
            TS(DVE, sc2a[:, l, :, :], mcol[:, l, 32:40, :], 1.0 / ALPHA, ALU.mult, ["mcol"], ["sc2a"])
        lam_init = lam_inits[0]
        DMA(SP, LAMT.rearrange("p a b -> p (a b)"), lam_d.rearrange("a b -> (a b)").rearrange("(o n) -> o n", o=1).broadcast_to([128, 256]),
            (), [("R2", "lamt")])
        TT(DVE, LAMT[:, 0, :], LAMT[:, 0, :], LAMT[:, 1, :], ALU.mult, [("R2", "lamt")], [("R2", "lamt")])
        TT(DVE, LAMT[:, 2, :], LAMT[:, 2, :], LAMT[:, 3, :], ALU.mult, [("R2", "lamt")], [("R2", "lamt")])
        P.op(DVE, lambda e: e.tensor_reduce(out=small[:, 0:1], in_=LAMT[:, 0, :], axis=AX.X, op=ALU.add), [("R2", "lamt")], ["sm01"])
        P.op(DVE, lambda e: e.tensor_reduce(out=small[:, 1:2], in_=LAMT[:, 2, :], axis=AX.X, op=ALU.add), [("R2", "lamt"), "sm01"], ["sm01"])
        ACTF(small[:, 2:4], small[:, 0:2], AF.Exp, ["sm01"], ["sm23"])
        TT(DVE, small[:, 4:5], small[:, 3:4], small[:, 2:3], ALU.subtract, ["sm23"], ["sm4"])
        TS(DVE, small[:, 5:6], small[:, 4:5], -lam_init, ALU.add, ["sm4"], ["neglam"])
        DMA(SP, gsub[:], subg_d.broadcast_to([128, 128]), (), ["gsub"])
        TS(DVE, gsub[:], gsub[:], 1.0 - lam_init, ALU.mult, ["gsub"], ["gsub"])

        class LnPipe:
            def __init__(self, final_eng=DVE):
                self.final_eng = final_eng
                self.p1 = []
                self.p2 = []

            def _A(self, slot):
                z = zt[:, slot, :]
                key = ("zt", slot)
                for c2 in range(2):
                    P.op(DVE, lambda e, c2=c2: e.bn_stats(out=stats[:, slot, c2, :], in_=z[:, c2 * 512:(c2 + 1) * 512]),
                         [key], [("stats", slot, c2)])
                P.op(DVE, lambda e: e.bn_aggr(out=mv[:, slot, 0:2], in_=stats[:, slot, :, :].rearrange("p a b -> p (a b)")),
                     [("stats", slot, 0), ("stats", slot, 1)], [("mv", slot, 0)])
                TS(DVE, mv[:, slot, 2:3], mv[:, slot, 1:2], LN_EPS, ALU.add, [("mv", slot, 0)], [("mv", slot, 1)])
                TT(POOL, mv[:, slot, 2:3], mv[:, slot, 2:3], negh[:, 0:1], ALU.pow, [("mv", slot, 1), "negh"], [("mv", slot, 1)])

            def _B(self, slot):
                z = zt[:, slot, :]
                key = ("zt", slot)
                STT(mv[:, slot, 3:4], mv[:, slot, 0:1], -1.0, mv[:, slot, 2:3], ALU.mult, ALU.mult,
                    [("mv", slot, 0), ("mv", slot, 1)], [("mv", slot, 2)])
                ACTF(z, z, AF.Identity, [key, ("mv", slot, 1), ("mv", slot, 2)], [key],
                     scale=mv[:, slot, 2:3], bias=mv[:, slot, 3:4])
                TT(POOL, z, z, TLG[:], ALU.mult, [key, "TLG"], [key])

            def _C(self, item):
                slot, dst, dst_key, after = item
                TT(self.final_eng, dst, zt[:, slot, :], TLB[:], ALU.add, [("zt", slot), "TLB"], [dst_key])
                if after is not None:
                    after()

            def push(self, slot, dst, dst_key, after=None):
                self._A(slot)
                if self.p1:
                    it_ = self.p1.pop(0)
                    self._B(it_[0])
                    self.p2.append(it_)
                if len(self.p2) > 1 or (self.p2 and not self.p1 and False):
                    self._C(self.p2.pop(0))
                self.p1.append((slot, dst, dst_key, after))

            def flush(self):
                while self.p1 or self.p2:
                    if self.p1:
                        it_ = self.p1.pop(0)
                        self._B(it_[0])
                        self.p2.append(it_)
                    if self.p2 and (len(self.p2) > 1 or not self.p1):
                        self._C(self.p2.pop(0))

        def load_bc(tile, src_row, key, eng=SP, reads=()):
            return DMA(eng, tile[:], src_row.broadcast_to([128, D]), reads, [key])

        def gd_keys(l_, b_, which):
            return [("gd", l_, b_, which, 0), ("gd", l_, b_, which, 1)]

        def transpose_mod(srcv, src_keys, nt, dstT, dst_key, scale_col, bias_col, col_keys):
            nblk = (nt + 3) // 4
            cnt = 0
            for bq in range(nblk):
                tl = list(range(bq * 4, min(nt, bq * 4 + 4)))
                for j in range(NCH):
                    bk = cnt % 2
                    cnt += 1
                    for ti, t_ in enumerate(tl):
                        TR(ps[:, bk, ti * 128:(ti + 1) * 128], srcv[:, t_, j * 128:(j + 1) * 128], ident[:],
                           [src_keys(t_), "ident"], [("ps", bk)])
                    w_ = len(tl) * 128
                    ACTF(dstT[:, j, tl[0] * 128:tl[0] * 128 + w_], ps[:, bk, 0:w_], AF.Identity,
                         [("ps", bk)] + col_keys, [dst_key(j, bq)], scale=scale_col(j), bias=bias_col(j))

        out_stores = []
        for b in range(2):
            P.switch("XR")
            P.switch("R2")
            l = 0
            MS(POOL, KA[64:128, :], 0.0, [("XR", "KAz")])
            MS(POOL, KBm[0:64, :], 0.0, [("XR", "KBz")])
            MS(POOL, VX[:, :, 128:136], 1.0, [("XR", "VXo")])
            DMA(SP, COS, cos_d, (), [("XR", "cos")])
            DMA(SP, SIN, sin_d, (), [("XR", "sin")])
            DMA(POOL, WO, wo_d.rearrange("(k p) n -> p k n", p=128), (), [("R2", "wo")])
            for i in range(NCT):
                ht_tile(b, "c", i, ["mcol"])
            if b > 0:
                for i in range(NT):
                    ht_tile(b, "x", i, ["mcol"])
            wv = wqkv_d.rearrange("(k p) n -> p k n", p=128)
            def load_head(hh):
                sl = hh % 2
                for wi in range(3):
                    DMA(POOL, WH2[:, sl, wi], wv[:, :, wi * D + hh * 128: wi * D + (hh + 1) * 128], (), [("XR", "wh", sl, wi)])
            load_head(0)
            deferred = []
            git = [0]
            for h in range(H):
                hs = h % 2
                WH = WH2[:, hs]
                if h + 1 < H:
                    load_head(h + 1)
                blocks = [(wsel, bq) for wsel in range(2) for bq in range(NQB)]

                def rope_s1(n):
                    wsel, bq = blocks[n]
                    tsl = slice(bq * QB, (bq + 1) * QB)
                    b0_ = 0 if n % 2 == 0 else 2
                    for kc in range(8):
                        MM(ps[:, b0_, 0:QB], WH[:, wsel, kc, :], hT[:, kc, tsl], kc == 0, kc == 7, [("XR", "wh", hs, wsel), ("hT", kc, bq)], [("ps", b0_)])
                    ACTF(QBF[:, n % 2, :], ps[:, b0_, 0:QB], AF.Identity, [("ps", b0_)], [("XR", "qbf", n % 2)])
                    ACTF(QF[:, n % 2, :], ps[:, b0_, 0:QB], AF.Identity, [("ps", b0_)], [("XR", "qf", n % 2)])

                def rope_s2(n):
                    wsel, bq = blocks[n]
                    tsl = slice(bq * QB, (bq + 1) * QB)
                    ksl = slice(NCT * 128 + bq * QB, NCT * 128 + (bq + 1) * QB)
                    b0_, b1_ = (0, 1) if n % 2 == 0 else (2, 3)
                    MM(ps[:, b1_, 0:QB], permb[:], QBF[:, n % 2, :], True, True, [("permb", 0), ("permb", 1), ("XR", "qbf", n % 2)], [("ps", b1_)])
                    TT(POOL, RT[:, 0, :], QF[:, n % 2, :], COS[:, tsl], ALU.mult, [("XR", "qf", n % 2), ("XR", "cos")], [("rt", 0)])
                    TT(DVE, RT[:, 1, :], ps[:, b1_, 0:QB], SIN[:, tsl], ALU.mult, [("ps", b1_), ("XR", "sin")], [("rt", 1)])
                    if wsel == 0:
                        TT(DVE, QT[:, tsl], RT[:, 0, :], RT[:, 1, :], ALU.add, [("rt", 0), ("rt", 1)], [("XR", "qt", bq)])
                    else:
                        TT(DVE, KA[0:64, ksl], RT[0:64, 0, :], RT[0:64, 1, :], ALU.add, [("rt", 0), ("rt", 1)], [("XR", "ka", bq)])
                        TT(DVE, KBm[64:128, ksl], RT[64:128, 0, :], RT[64:128, 1, :], ALU.add, [("rt", 0), ("rt", 1)], [("XR", "kb", bq)])
                for n in range(len(blocks)):
                    rope_s1(n)
                    if n >= 1:
                        rope_s2(n - 1)
                rope_s2(len(blocks) - 1)
                for kc in range(8):
                    MM(ps[:, 0, 0:CTX], WH[:, 1, kc, :], hcT[:, kc, :], kc == 0, kc == 7, [("XR", "wh", hs, 1), ("R2", "hcT", kc)], [("ps", 0)])
                ACTF(KA[0:64, 0:CTX], ps[0:64, 0, 0:CTX], AF.Identity, [("ps", 0)], [("XR", "ka", "c")])
                ACTF(KBm[64:128, 0:CTX], ps[64:128, 0, 0:CTX], AF.Identity, [("ps", 0)], [("XR", "kb", "c")])
                for g0 in range(0, KT, 4):
                    tl = list(range(g0, min(KT, g0 + 4)))
                    bk = 1 + (g0 // 4) % 3
                    for ti, kt in enumerate(tl):
                        for kc in range(8):
                            if kt < NCT:
                                lt, rk = hcT[:, kc, kt * 128:(kt + 1) * 128], ("R2", "hcT", kc)
                            else:
                                tt = kt - NCT
                                lt, rk = hT[:, kc, tt * 128:(tt + 1) * 128], ("hT", kc, tt // TPB)
                            MM(ps[:, bk, ti * 128:(ti + 1) * 128], lt, WH[:, 2, kc, :], kc == 0, kc == 7, [("XR", "wh", hs, 2), rk], [("ps", bk)])
                    ACTF(VX[:, tl[0]:tl[0] + len(tl), 0:128], ps[:, bk, 0:len(tl) * 128].rearrange("p (t e) -> p t e", e=128), AF.Identity,
                         [("ps", bk)], [("XR", "vx", g0)])
                its = [(bq, kt) for bq in range(NQB) for kt in range(KT)]

                def scores(it):
                    bq, kt = its[it]
                    sset = it % 2
                    qsl = slice(bq * QB, (bq + 1) * QB)
                    kk = slice(kt * 128, (kt + 1) * 128)
                    krd_a = [("XR", "KAz"), ("XR", "ka", "c") if kt < NCT else ("XR", "ka", (kt - NCT) // TPB)]
                    krd_b = [("XR", "KBz"), ("XR", "kb", "c") if kt < NCT else ("XR", "kb", (kt - NCT) // TPB)]
                    MM(ps[:, 2 * sset, 0:QB], KA[:, kk], QT[:, qsl], True, True, krd_a + [("XR", "qt", bq)], [("ps", 2 * sset)])
                    MM(ps[:, 2 * sset + 1, 0:QB], KBm[:, kk], QT[:, qsl], True, True, krd_b + [("XR", "qt", bq)], [("ps", 2 * sset + 1)])

                def finish_block(h, bq):
                    qsl = slice(bq * QB, (bq + 1) * QB)
                    psb = ps[:, 7, :].bitcast(BF16)
                    for j in range(TPB):
                        TR(psb[:, j * 128:(j + 1) * 128], onb[:, j, :], identb[:], [("onb", j), "identb"], [("ps", 7)])
                    ACTF(onT[:, h, qsl], psb[:, 0:QB], AF.Identity, [("ps", 7)], [("R2", "onT", h, bq)])

                scores(0)
                for it in range(len(its)):
                    bq, kt = its[it]
                    sset = it % 2
                    if it + 1 < len(its):
                        scores(it + 1)
                    git[0] += 1
                    if deferred and deferred[0][0] <= git[0]:
                        finish_block(*deferred.pop(0)[1])
                    ACTF(EB[:, sset], ps[:, 2 * sset:2 * sset + 2, 0:QB], AF.Exp, [("ps", 2 * sset), ("ps", 2 * sset + 1)],
                         [("XR", "eb", sset)], scale=0.125)
                    for c in range(2):
                        for j in range(TPB):
                            a = c * TPB + j
                            bkk = 4 + a // 3
                            o_ = (a % 3) * 160
                            first_in_bank = (a % 3 == 0)
                            MM(ps[:, bkk, o_:o_ + 129], EB[:, sset, c, j * 128:(j + 1) * 128], VX[:, kt, 0:129],
                               (kt == 0) and first_in_bank, kt == KT - 1,
                               [("XR", "eb", sset), ("XR", "vx", (kt // 4) * 4), ("XR", "VXo")], [("ps", bkk)], skip=True)
                    if kt != KT - 1:
                        continue
                    nb = (NACC + 2) // 3
                    for bb in range(nb):
                        na = min(3, NACC - 3 * bb)
                        CP(DVE, accS[:, bb, 0:na * 130].rearrange("p (a e) -> p a e", e=130)[:, :, 0:129],
                           ps[:, 4 + bb, 0:na * 160].rearrange("p (a e) -> p a e", e=160)[:, :, 0:129], [("ps", 4 + bb)], [("accS", bb)])
                    accf = accS[:].rearrange("p b n -> p (b n)")

                    def AV_(a):
                        base = (a // 3) * 390 + (a % 3) * 130
                        return accf[:, base:base + 128], accf[:, base + 128:base + 129]
                    akeys = [("accS", bb) for bb in range(nb)]
                    for a in range(NACC):
                        P.op(DVE, lambda e, a=a: e.reciprocal(out=small[:, 8 + a:9 + a], in_=AV_(a)[1]), akeys, [("rr", a)])
                    for j in range(TPB):
                        TT(DVE, small[:, 8 + TPB + j:9 + TPB + j], small[:, 8 + TPB + j:9 + TPB + j], small[:, 5:6], ALU.mult,
                           [("rr", TPB + j), "neglam"], [("rr", TPB + j)])
                    for j in range(TPB):
                        TS(DVE, ework[:, 0, j * 128:(j + 1) * 128], AV_(TPB + j)[0], small[:, 8 + TPB + j:9 + TPB + j], ALU.mult,
                           akeys + [("rr", TPB + j)], [("ew", 0, j)])
                    for j in range(TPB):
                        STT(ework[:, 1, j * 128:(j + 1) * 128], AV_(j)[0], small[:, 8 + j:9 + j], ework[:, 0, j * 128:(j + 1) * 128],
                            ALU.mult, ALU.add, akeys + [("rr", j), ("ew", 0, j)], [("ew", 1, j)])
                    for j in range(TPB):
                        TT(DVE, ework[:, 2, j * 128:(j + 1) * 128], ework[:, 1, j * 128:(j + 1) * 128], ework[:, 1, j * 128:(j + 1) * 128],
                           ALU.mult, [("ew", 1, j)], [("ew", 2, j)])
                    P.op(DVE, lambda e: e.tensor_reduce(out=small[:, 24:24 + TPB], in_=ework[:, 2, 0:TPB * 128].rearrange("p (j e) -> p j e", e=128),
                                                        axis=AX.X, op=ALU.add), [("ew", 2, j) for j in range(TPB)], ["ss"])
                    TS(DVE, small[:, 24:24 + TPB], small[:, 24:24 + TPB], 1.0 / 128.0, ALU.mult, ["ss"], ["ss"], s2=LN_EPS, op1=ALU.add)
                    TT(POOL, small[:, 28:28 + TPB], small[:, 24:24 + TPB], negh[:, 0:TPB], ALU.pow, ["ss", "negh"], ["rstd"])
                    while deferred:
                        finish_block(*deferred.pop(0)[1])
                    for j in range(TPB):
                        STT(onb[:, j, :], ework[:, 1, j * 128:(j + 1) * 128], small[:, 28 + j:29 + j], gsub[:], ALU.mult, ALU.mult,
                            [("ew", 1, j), "rstd", "gsub"], [("onb", j)])
                    deferred.append((git[0] + 12, (h, bq)))
            while deferred:
                finish_block(*deferred.pop(0)[1])
            P.switch("XR")
            load_bc(TG, gd_d[0, b, 0:1, :], "TG", reads=gd_keys(0, b, 0))
            load_bc(TLG, lng_d[0, 0:1, :], "TLG"); load_bc(TLB, lnb_d[0, 0:1, :], "TLB")
            TS(DVE, TLG[:], TLG[:], ALPHA, ALU.mult, ["TLG"], ["TLG"])
            TS(DVE, TLB[:], TLB[:], ALPHA, ALU.mult, ["TLB"], ["TLB"])
            lnp = LnPipe()
            for i in range(NT):
                slot = i % 3
                bq = i // TPB
                DMA(SP, XRv[:, i, :], x_d[b, i * 128:(i + 1) * 128, :], (), [("XR", "x", i)])
                bks = (0, 1) if i % 2 == 0 else (2, 3)
                for hf in range(2):
                    for h in range(H):
                        MM(ps[:, bks[hf], :], onT[:, h, i * 128:(i + 1) * 128], WO[:, h, hf * 512:(hf + 1) * 512], h == 0, h == H - 1,
                           [("R2", "onT", h, bq), ("R2", "wo")], [("ps", bks[hf])])
                z = zt[:, slot, :]
                TT(DVE, z, ps[:, bks[0]:bks[0] + 2, :].rearrange("p a n -> p (a n)"), TG[:], ALU.mult,
                   [("ps", bks[0]), ("ps", bks[1]), "TG"], [("zt", slot)])
                STT(z, XRv[:, i, :], ALPHA, z, ALU.mult, ALU.add, [("XR", "x", i), ("zt", slot)], [("zt", slot)])
                lnp.push(slot, XRv[:, i, :], ("XR", "x", i))
            lnp.flush()

            for l in range(2):
                if l == 1:
                    P.switch("R2")
                    DMA(SP, PWF, poolw_d.rearrange("g (c p) n -> p g c n", p=128), (), [("R2", "pwf")])
                    DMA(POOL, BAND, band_d.rearrange("g v p n -> p g v n"), (), [("R2", "band")])
                    DMA(SP, INVE.rearrange("p g e n -> p (g e n)"),
                        inve_d.rearrange("g e n -> (g e n)").rearrange("(o n) -> o n", o=1).broadcast_to([128, 8 * 128]), (), [("R2", "inve")])
                    load_bc(TG, gd_d[1, b, 0:1, :], "TG", reads=gd_keys(1, b, 0))
                    DMA(SP, zt[:, 0, :], pscale_d.broadcast_to([128, D]), (), [("zt", 0)])
                    TT(DVE, TG[:], TG[:], zt[:, 0, :], ALU.mult, ["TG", ("zt", 0)], ["TG"])
                    for ci in range(2):
                        TT(DVE, PW[:, :, ci, :], PWF[:, :, ci, :], TG[:].rearrange("p (g o) -> p g o", g=4), ALU.mult,
                           [("R2", "pwf"), "TG"], [("R2", "pw", ci)])
                    load_bc(TLG, lng_d[1, 0:1, :], "TLG"); load_bc(TLB, lnb_d[1, 0:1, :], "TLB")
                    TS(DVE, TLG[:], TLG[:], ALPHA, ALU.mult, ["TLG"], ["TLG"])
                    TS(DVE, TLB[:], TLB[:], ALPHA, ALU.mult, ["TLB"], ["TLB"])

                    def split_tile(i):
                        s4 = i % 4
                        ACTF(XH[:, s4, 0, :], XRv[:, i, :], AF.Identity, [("XR", "x", i)], [("R2", "xh", s4, 0)])
                        TT(DVE, XH[:, s4, 1, :], XRv[:, i, :], XH[:, s4, 0, :], ALU.subtract, [("XR", "x", i), ("R2", "xh", s4, 0)], [("R2", "xh", s4, 1)])
                    split_tile(0)
                    if NT > 1:
                        split_tile(1)
                    pcnt = 0
                    lnp = LnPipe(POOL)
                    def pooled_tile(i):
                        nonlocal pcnt
                        for ch in range(NCH):
                            g = ch // 2
                            bk = 4 + (pcnt % 4)
                            pcnt += 1
                            srcs = [s_ for s_ in (i - 1, i, i + 1) if 0 <= s_ < NT]
                            nmm = len(srcs) * 2
                            m_ = 0
                            for s_ in srcs:
                                rel = s_ - i
                                if rel == 0:
                                    v = 3 if i == 0 else (4 if i == NT - 1 else 1)
                                else:
                                    v = 0 if rel == -1 else 2
                                for a_ in range(2):
                                    MM(ps[:, bk, 0:128], XH[:, s_ % 4, a_, ch * 128:(ch + 1) * 128], BAND[:, g, v, :], m_ == 0, m_ == nmm - 1,
                                       [("R2", "xh", s_ % 4, a_), ("R2", "band")], [("ps", bk)])
                                    m_ += 1
                            if i == 0 or i == NT - 1:
                                STT(hT[:, ch, i * 128:(i + 1) * 128], ps[:, bk, 0:128], mcol[:, 1, 8 + ch, b:b + 1], INVE[:, g, 0 if i == 0 else 1, :],
                                    ALU.mult, ALU.mult, [("ps", bk), "mcol", ("R2", "inve")], [("hTt", ch, i), ("hT", ch, i // TPB)])
                            else:
                                TS(DVE, hT[:, ch, i * 128:(i + 1) * 128], ps[:, bk, 0:128], mcol[:, 1, 8 + ch, b:b + 1], ALU.mult,
                                   [("ps", bk), "mcol"], [("hTt", ch, i), ("hT", ch, i // TPB)], s2=1.0 / WINDOWS[g], op1=ALU.mult)

                    def mix_tile(i):
                        slot = i % 3
                        bks = (0, 1) if i % 2 == 0 else (2, 3)
                        for g in range(4):
                            for ci in range(2):
                                MM(ps[:, bks[g // 2], (g % 2) * 256:(g % 2 + 1) * 256], hT[:, 2 * g + ci, i * 128:(i + 1) * 128], PW[:, g, ci, :],
                                   ci == 0, ci == 1, [("hTt", 2 * g + ci, i), ("R2", "pw", ci)], [("ps", bks[g // 2])])
                        STT(zt[:, slot, :], XRv[:, i, :], ALPHA, ps[:, bks[0]:bks[0] + 2, :].rearrange("p a n -> p (a n)"), ALU.mult, ALU.add,
                            [("XR", "x", i), ("ps", bks[0]), ("ps", bks[1])], [("zt", slot)])
                        lnp.push(slot, XRv[:, i, :], ("XR", "x", i))

                    for i in range(NT + 1):
                        if i < NT:
                            pooled_tile(i)
                            if i + 2 < NT:
                                split_tile(i + 2)
                        if i >= 1:
                            mix_tile(i - 1)
                    lnp.flush()
                P.switch("R2")
                load_bc(TG, gd_d[l, b, 1:2, :], "TG", reads=gd_keys(l, b, 1))
                load_bc(TLG, lng_d[l, 1:2, :], "TLG"); load_bc(TLB, lnb_d[l, 1:2, :], "TLB")
                cnt = 0
                for bq in range(NQB):
                    for j in range(NCH):
                        bk = (cnt // 2) % 2 + (0 if j % 2 == 0 else 2)
                        cnt += 1
                        for ti in range(TPB):
                            t_ = bq * TPB + ti
                            TR(ps[:, bk, ti * 128:(ti + 1) * 128], XRv[:, t_, j * 128:(j + 1) * 128], ident[:], [("XR", "x", t_), "ident"], [("ps", bk)])
                        hk = [("hT", j, bq)] + ([("hTt", j, bq * TPB + ti) for ti in range(TPB)] if l == 1 else [])
                        if j % 2 == 0:
                            ACTF(hT[:, j, bq * QB:(bq + 1) * QB], ps[:, bk, 0:QB], AF.Identity, [("ps", bk), "mcol", "sc2a"], hk,
                                 scale=sc2a[:, l, j, b:b + 1], bias=mcol[:, l, 24 + j, b:b + 1])
                        else:
                            TS(DVE, hT[:, j, bq * QB:(bq + 1) * QB], ps[:, bk, 0:QB], sc2a[:, l, j, b:b + 1], ALU.mult,
                               [("ps", bk), "mcol", "sc2a"], hk, s2=mcol[:, l, 24 + j, b:b + 1], op1=ALU.add)
                winv = win_d[l].rearrange("(k p) n -> p k n", p=128)
                gi = 0
                lnp2 = LnPipe()
                for grp in groups:
                    s_ = gi % 2
                    gi += 1
                    ng = len(grp)
                    c0 = grp[0]
                    DMA(POOL, WIN[:, s_, 0, :, 0:ng * 128], winv[:, :, c0 * 128:(c0 + ng) * 128], (), [("R2", "win", s_, 0)])
                    DMA(POOL, WIN[:, s_, 1, :, 0:ng * 128], winv[:, :, DFF + c0 * 128:DFF + (c0 + ng) * 128], (), [("R2", "win", s_, 1)])
                    for ci, c in enumerate(grp):
                        ws = (gi * GC + ci) % 2
                        DMA(SP, WST[:, ws, :], wout_d[l, c * 128:(c + 1) * 128, :], (), [("R2", "wst", ws)])
                        TT(DVE, WOUT[:, s_, ci, :], WST[:, ws, :], TG[:], ALU.mult, [("R2", "wst", ws), "TG"], [("R2", "wout", s_, ci)])
                    for bq in range(NQB):
                        tsl = slice(bq * QB, (bq + 1) * QB)
                        as_ = bq % 2
                        for ci, c in enumerate(grp):
                            sgs = ci % 2
                            bg, bu = (0, 1) if (ci % 2 == 0) else (2, 3)
                            for kc in range(8):
                                MM(ps[:, bg, 0:QB], WIN[:, s_, 0, kc, ci * 128:(ci + 1) * 128], hT[:, kc, tsl], kc == 0, kc == 7,
                                   [("R2", "win", s_, 0), ("hT", kc, bq)], [("ps", bg)])
                            for kc in range(8):
                                MM(ps[:, bu, 0:QB], WIN[:, s_, 1, kc, ci * 128:(ci + 1) * 128], hT[:, kc, tsl], kc == 0, kc == 7,
                                   [("R2", "win", s_, 1), ("hT", kc, bq)], [("ps", bu)])
                            ACTF(SG[:, sgs, :], ps[:, bg, 0:QB], AF.Silu, [("ps", bg)], [("R2", "sg", sgs)])
                            TT(DVE, AT[:, as_, ci, :], ps[:, bu, 0:QB], SG[:, sgs, :], ALU.mult, [("ps", bu), ("R2", "sg", sgs)], [("R2", "at", as_, ci)])
                        for ti in range(TPB):
                            t_ = bq * TPB + ti
                            bo = (4, 5) if (ti % 2 == 0) else (6, 7)
                            for hf in range(2):
                                for ci in range(ng):
                                    MM(ps[:, bo[hf], :], AT[:, as_, ci, ti * 128:(ti + 1) * 128], WOUT[:, s_, ci, hf * 512:(hf + 1) * 512],
                                       ci == 0, ci == ng - 1, [("R2", "at", as_, ci), ("R2", "wout", s_, ci)], [("ps", bo[hf])])
                            if grp is not groups[-1]:
                                TT(DVE, XRv[:, t_, :], ps[:, bo[0]:bo[0] + 2, :].rearrange("p a n -> p (a n)"), XRv[:, t_, :], ALU.add,
                                   [("ps", bo[0]), ("ps", bo[1]), ("XR", "x", t_)], [("XR", "x", t_)])
                            else:
                                slot = t_ % 3
                                TT(DVE, zt[:, slot, :], ps[:, bo[0]:bo[0] + 2, :].rearrange("p a n -> p (a n)"), XRv[:, t_, :], ALU.add,
                                   [("ps", bo[0]), ("ps", bo[1]), ("XR", "x", t_)], [("zt", slot)])
                                def store_(t_=t_, b=b):
                                    out_stores.append(DMA(SP, out_d[b, t_ * 128:(t_ + 1) * 128, :], XRv[:, t_, :], [("XR", "x", t_)],
                                                          [("out", b, t_)], semkey=("ost", t_ % 4)))
                                lnp2.push(slot, XRv[:, t_, :], ("XR", "x", t_), store_ if l == 1 else None)
                lnp2.flush()
        P.emit(final_waits=out_stores[-8:])
    return nc


_CACHE = {}


def make_core_inputs(inp, core, S, CTX):
    f = lambda a: np.ascontiguousarray(np.asarray(a, dtype=np.float32))
    b0 = 2 * core
    cos128, sin128 = rope_tables(S)
    band, invedge = pool_tables(S)
    return {
        "x": f(inp["x"][b0:b0 + 2]),
        "cvec": f(np.concatenate([np.asarray(inp["c"])[b0:b0 + 2], np.asarray(inp["c_ctx"])[None, :]], axis=0)),
        "ctx": f(inp["ctx"][b0:b0 + 2]),
        "w_mod": f(inp["w_mod"]), "b_mod": f(inp["b_mod"]),
        "ln_g": f(inp["ln_g"]), "ln_b": f(inp["ln_b"]),
        "w_qkv": f(inp["attn_w_qkv"][0]), "lam": f(inp["attn_lambda"][0]),
        "subg": f(np.asarray(inp["attn_subln_g"])[0][None, :]),
        "w_o": f(inp["attn_w_o"][0]), "pool_w": f(inp["pool_w"][0]),
        "pool_scale": f(np.asarray(inp["pool_scale"])[0][None, :]),
        "w_in": f(inp["ffn_w_in"]), "w_out": f(inp["ffn_w_out"]),
        "rope_cos": cos128, "rope_sin": sin128, "band": band, "invedge": invedge,
    }


def kernel(**inputs):
    S = inputs["x"].shape[1]
    CTX = inputs["ctx"].shape[1]
    DFF = inputs["ffn_w_out"].shape[1]
    B = inputs["x"].shape[0]
    ncores = B // 2
    key = (S, CTX, DFF)
    if key not in _CACHE:
        _CACHE[key] = build(S, CTX, DFF)
    nc = _CACHE[key]
    in_maps = [make_core_inputs(inputs, c, S, CTX) for c in range(ncores)]
    res = run_bass_kernel_spmd(nc, in_maps, core_ids=list(range(ncores)))
    out = np.concatenate([np.asarray(r["out"]) for r in res.results], axis=0)
    return out.astype(np.float32)
```

```python
import math
import contextlib
import numpy as np
import concourse.bass as bass
import concourse.mybir as mybir
from concourse.bass_utils import run_bass_kernel_spmd

F32 = mybir.dt.float32
BF16 = mybir.dt.bfloat16
AF = mybir.ActivationFunctionType
ALU = mybir.AluOpType
AX = mybir.AxisListType
PE, ACT, DVE, POOL, SP = "tensor", "scalar", "vector", "gpsimd", "sync"
ENGS = (PE, ACT, DVE, POOL, SP)

D = 1024
NCH = 8
H = 8
ALPHA = (2.0 * 2) ** 0.25
LN_EPS = 1e-5
WINDOWS = (2, 4, 8, 16)
GRID_W = 64


class Rec:
    __slots__ = ("eng", "fn", "deps", "sig", "count", "dma", "semkey", "cum", "idx")


class Prog:
    def __init__(self, nc):
        self.nc = nc
        self.recs = {e: [] for e in ENGS}
        self.last_w = {}
        self.readers = {}
        self.dma_cum = {}
        self.n = 0
        self.pending = {}
        self.seen = {}

    def switch(self, region):
        lat = {}
        dmas = []
        old = self.pending.get(region, [])
        cand = list(old)
        for k in list(self.last_w.keys()):
            if isinstance(k, tuple) and k[0] == region:
                cand.append(self.last_w.pop(k))
        for k in list(self.readers.keys()):
            if isinstance(k, tuple) and k[0] == region:
                cand.extend(self.readers.pop(k))
        for r in cand:
            if r.dma:
                dmas.append(r)
            else:
                if r.eng not in lat or lat[r.eng].idx < r.idx:
                    lat[r.eng] = r
        self.pending[region] = list(lat.values()) + dmas
        self.seen[region] = set()

    def op(self, eng, fn, reads=(), writes=(), dma=False, semkey=None):
        r = Rec()
        r.eng, r.fn, r.dma, r.sig, r.count, r.idx = eng, fn, dma, False, 0, self.n
        self.n += 1
        deps = set()
        for k in tuple(reads) + tuple(writes):
            if isinstance(k, tuple) and k[0] in self.pending and k not in self.seen[k[0]]:
                self.seen[k[0]].add(k)
                deps.update(self.pending[k[0]])
        for k in reads:
            w = self.last_w.get(k)
            if w is not None:
                deps.add(w)
        for k in writes:
            w = self.last_w.get(k)
            if w is not None:
                deps.add(w)
            deps.update(self.readers.get(k, ()))
        for k in reads:
            self.readers.setdefault(k, []).append(r)
        for k in writes:
            self.last_w[k] = r
            self.readers[k] = []
        deps.discard(r)
        r.deps = deps
        if dma:
            r.semkey = semkey if semkey is not None else writes[0]
            self.dma_cum[r.semkey] = self.dma_cum.get(r.semkey, 0) + 16
            r.cum = self.dma_cum[r.semkey]
        else:
            r.semkey, r.cum = None, 0
        self.recs[eng].append(r)
        return r

    def emit(self, final_waits=()):
        nc = self.nc
        for e in ENGS:
            for r in self.recs[e]:
                for d in r.deps:
                    if d.dma:
                        continue
                    if d.eng == PE and r.eng == PE and not r.dma:
                        continue
                    d.sig = True
        for r in final_waits:
            if not r.dma:
                r.sig = True
        for e in ENGS:
            c = 0
            for r in self.recs[e]:
                if r.dma:
                    continue
                if r.sig:
                    c += 1
                    r.count = c
        with contextlib.ExitStack() as st:
            esem = {e: st.enter_context(nc.semaphore("p_" + e)) for e in ENGS}
            dsem = {}
            for i, k in enumerate(self.dma_cum):
                dsem[k] = st.enter_context(nc.semaphore("d%d" % i))
            block = st.enter_context(nc.Block())

            def make(e):
                def body(eng):
                    waited = {}

                    def wait(sem, val, key):
                        if waited.get(key, 0) >= val:
                            return
                        waited[key] = val
                        eng.wait_ge(sem, val)

                    for r in self.recs[e]:
                        tg = {}
                        for d in r.deps:
                            if d.dma:
                                k_ = ("d", d.semkey)
                                if tg.get(k_, (None, 0))[1] < d.cum:
                                    tg[k_] = (dsem[d.semkey], d.cum)
                            else:
                                if d.eng == PE and e == PE and not r.dma:
                                    continue
                                k_ = ("e", d.eng)
                                if tg.get(k_, (None, 0))[1] < d.count:
                                    tg[k_] = (esem[d.eng], d.count)
                        for k_ in sorted(tg, key=str):
                            wait(tg[k_][0], tg[k_][1], k_)
                        ins = r.fn(eng)
                        if r.dma:
                            ins.then_inc(dsem[r.semkey], 16)
                        elif r.sig:
                            ins.then_inc(esem[e], 1)
                    if e == SP:
                        tg = {}
                        for r in final_waits:
                            if r.dma:
                                k_ = ("d", r.semkey)
                                if tg.get(k_, (None, 0))[1] < r.cum:
                                    tg[k_] = (dsem[r.semkey], r.cum)
                            else:
                                k_ = ("e", r.eng)
                                if tg.get(k_, (None, 0))[1] < r.count:
                                    tg[k_] = (esem[r.eng], r.count)
                        for k_ in sorted(tg, key=str):
                            wait(tg[k_][0], tg[k_][1], k_)
                return body

            for e in ENGS:
                getattr(block, e)(make(e))


def rope_tables(S):
    rows = S // GRID_W
    t = np.arange(S)
    row = (t // GRID_W).astype(np.float32)
    col = (t % GRID_W).astype(np.float32)
    inv = np.power(np.float32(10000.0), -np.arange(16, dtype=np.float32) / np.float32(16)).astype(np.float32)
    ang = np.concatenate([row[:, None] * inv, col[:, None] * inv], axis=-1).astype(np.float32)
    cos = np.cos(ang).astype(np.float32).T
    sin = np.sin(ang).astype(np.float32).T
    cos128 = np.tile(cos, (4, 1))
    sin128 = np.concatenate([-sin, sin, -sin, sin], axis=0)
    return np.ascontiguousarray(cos128), np.ascontiguousarray(sin128)


def pool_tables(S):
    NT = S // 128
    band = np.zeros((4, 5, 128, 128), np.float32)
    invedge = np.zeros((4, 2, 128), np.float32)

    def full(w):
        t = np.arange(S)
        lo = np.clip(t - w // 2, 0, S)
        hi = np.clip(t + w - w // 2, 0, S)
        return lo, hi

    for g, w in enumerate(WINDOWS):
        lo, hi = full(w)
        cnt = (hi - lo)

        def blockmat(ti, si):
            m = np.zeros((128, 128), np.float32)
            for tt in range(128):
                t = ti * 128 + tt
                for tp in range(lo[t], hi[t]):
                    if si * 128 <= tp < (si + 1) * 128:
                        m[tp - si * 128, tt] += 1.0
                if si == ti:
                    m[tt, tt] -= cnt[t]
            return m
        mid = 1 if NT > 2 else 0
        band[g, 0] = blockmat(mid, mid - 1) if mid >= 1 else 0
        band[g, 1] = blockmat(mid, mid) if NT > 2 else 0
        band[g, 2] = blockmat(mid, mid + 1) if mid + 1 < NT else 0
        band[g, 3] = blockmat(0, 0)
        band[g, 4] = blockmat(NT - 1, NT - 1)
        if NT <= 2:
            band[g, 0] = blockmat(1, 0)
            band[g, 2] = blockmat(0, 1)
        invedge[g, 0] = 1.0 / cnt[0:128]
        invedge[g, 1] = 1.0 / cnt[S - 128:S]
    return band, invedge


def build(S=2048, CTX=256, DFF=2816):
    NT = S // 128
    NCT = CTX // 128
    KT = NCT + NT
    QB = min(512, S)
    NQB = S // QB
    TPB = QB // 128
    NFC = DFF // 128
    GC = 4
    groups = [list(range(a, min(a + GC, NFC))) for a in range(0, NFC, GC)]
    NACC = 2 * TPB
    lam_inits = [0.8 - 0.6 * math.exp(-0.3 * 0)]

    nc = bass.Bass("TRN2", target_bir_lowering=False)
    dr = lambda n, s, k="ExternalInput": nc.dram_tensor(n, s, F32, kind=k).ap()
    x_d = dr("x", [2, S, D]); cv_d = dr("cvec", [3, D]); ctx_d = dr("ctx", [2, CTX, D])
    wmod_d = dr("w_mod", [2, D, 6 * D]); bmod_d = dr("b_mod", [2, 6 * D])
    lng_d = dr("ln_g", [2, 2, D]); lnb_d = dr("ln_b", [2, 2, D])
    wqkv_d = dr("w_qkv", [D, 3 * D]); lam_d = dr("lam", [4, 64]); subg_d = dr("subg", [1, 128])
    wo_d = dr("w_o", [D, D]); poolw_d = dr("pool_w", [4, 256, 256]); pscale_d = dr("pool_scale", [1, D])
    win_d = dr("w_in", [2, D, 2 * DFF]); wout_d = dr("w_out", [2, DFF, D])
    cos_d = dr("rope_cos", [128, S]); sin_d = dr("rope_sin", [128, S])
    band_d = dr("band", [4, 5, 128, 128]); inve_d = dr("invedge", [4, 2, 128])
    out_d = dr("out", [2, S, D], "ExternalOutput")
    gd_d = dr("gscratch", [2, 2, 2, D], "Internal")

    st = contextlib.ExitStack()
    with st:
        sb = lambda n, s, d: st.enter_context(nc.sbuf_tensor(n, s, d))
        XR = sb("XR", [128, 16 * 1024], F32)
        hT = sb("hT", [128, NCH, S], BF16)
        R2 = sb("R2", [128, 17 * 1024], F32)
        TG = sb("TG", [128, D], F32); TLG = sb("TLG", [128, D], F32)
        TLB = sb("TLB", [128, D], F32)
        ident = sb("ident", [128, 128], F32)
        identb = sb("identb", [128, 128], BF16)
        permb = sb("permb", [128, 128], BF16)
        mcol = sb("mcol", [128, 2, 48, 4], F32)
        sc2a = sb("sc2a", [128, 2, 8, 4], F32)
        scT = sb("scT", [128, 9, 4], BF16)
        small = sb("small", [128, 64], F32)
        stats = sb("stats", [128, 3, 2, 6], F32)
        mv = sb("mv", [128, 3, 4], F32)
        gsub = sb("gsub", [128, 128], F32)
        ework = sb("ework", [128, 3, 4 * 128], F32)
        accS = sb("accS", [128, 3, 390], F32)
        onb = sb("onb", [128, 4, 128], BF16)
        zt = sb("zt", [128, 3, D], F32)
        negh = sb("negh", [128, 8], F32)
        RT = sb("RT", [128, 2, QB], F32)
        ps = st.enter_context(nc.psum_tensor("ps", [128, 8, 512], F32))
        P = Prog(nc)

        XRv = XR[:].rearrange("p (t d) -> p t d", d=D) if NT == 16 else XR[:, 0:NT * D].rearrange("p (t d) -> p t d", d=D)
        XRb = XR[:].bitcast(BF16)
        XRf = XR[:]
        off = [0]

        def carve_b(n):
            a = XRb[:, off[0]:off[0] + n]
            off[0] += n
            return a
        QT = carve_b(S)
        KA = carve_b(KT * 128); KBm = carve_b(KT * 128)
        VX = carve_b(KT * 136).rearrange("p (k e) -> p k e", e=136)
        WH2 = carve_b(2 * 3 * 1024).rearrange("p (s w k n) -> p s w k n", s=2, w=3, k=8)
        QBF = carve_b(2 * QB).rearrange("p (s q) -> p s q", s=2)
        QF = carve_b(4 * QB).bitcast(F32).rearrange("p (s q) -> p s q", s=2)
        EB = carve_b(2 * 2 * QB).rearrange("p (s c q) -> p s c q", s=2, c=2)
        assert off[0] % 2 == 0
        foff = off[0] // 2
        COS = XRf[:, foff:foff + S]; SIN = XRf[:, foff + S:foff + 2 * S]
        foff += 2 * S
        assert foff <= 16 * 1024, foff

        R2b = R2[:].bitcast(BF16)
        R2f = R2[:]
        onT = R2b[:, 0:NCH * S].rearrange("p (h s) -> p h s", h=NCH)
        WO = R2b[:, 16384:16384 + 8 * D].rearrange("p (k n) -> p k n", k=8)
        hcT = R2b[:, 24576:24576 + NCH * CTX].rearrange("p (c s) -> p c s", c=NCH)
        LAMT = R2f[:, 14336:14336 + 256].rearrange("p (a b) -> p a b", a=4)
        CVT = R2f[:, 0:1024]
        WM = R2b[:, 2048:2048 + 2 * 9 * 1024].rearrange("p (s k n) -> p s k n", s=2, k=9)
        GROW = R2f[:, 10240:10240 + 2 * 512].rearrange("p (s n) -> p s n", s=2)
        SCB = R2b[:, 22528:22528 + 9 * 2 * 128].rearrange("p (k n m) -> p k n m", k=9, n=2)
        WIN = R2b[:, 0:2 * 2 * 8 * GC * 128].rearrange("p (s u k n) -> p s u k n", s=2, u=2, k=8)
        o2 = 2 * 2 * 8 * GC * 128
        WOUT = R2b[:, o2:o2 + 2 * GC * D].rearrange("p (s c n) -> p s c n", s=2, c=GC)
        o2 += 2 * GC * D
        AT = R2b[:, o2:o2 + 2 * GC * QB].rearrange("p (s c q) -> p s c q", s=2, c=GC)
        o2 += 2 * GC * QB
        assert o2 % 2 == 0
        f2 = o2 // 2
        WST = R2f[:, f2:f2 + 2 * D].rearrange("p (s n) -> p s n", s=2)
        f2 += 2 * D
        SG = R2f[:, f2:f2 + 2 * QB].rearrange("p (s q) -> p s q", s=2)
        f2 += 2 * QB
        assert f2 <= 17 * 1024, f2
        XH = R2b[:, 0:4 * 2 * D].rearrange("p (s a d) -> p s a d", s=4, a=2)
        PW = R2b[:, 8192:8192 + 4 * 2 * 256].rearrange("p (g c n) -> p g c n", g=4, c=2)
        BAND = R2b[:, 10240:10240 + 20 * 128].rearrange("p (g v n) -> p g v n", g=4, v=5)
        INVE = R2f[:, 6400:6400 + 8 * 128].rearrange("p (g e n) -> p g e n", g=4, e=2)
        PWF = R2f[:, 7424:7424 + 4 * 2 * 256].rearrange("p (g c n) -> p g c n", g=4, c=2)

        def MM(out, lhsT, rhs, start, stop, r, w, skip=False):
            return P.op(PE, lambda e: e.matmul(out, lhsT=lhsT, rhs=rhs, start=start, stop=stop, skip_group_check=skip), r, w)

        def TR(out, in_, idn, r, w):
            return P.op(PE, lambda e: e.transpose(out, in_, idn), r, w)

        def ACTF(out, in_, func, r, w, scale=1.0, bias=0.0):
            return P.op(ACT, lambda e: e.activation(out=out, in_=in_, func=func, scale=scale, bias=bias), r, w)

        def TT(eng, out, a, b_, op, r, w):
            return P.op(eng, lambda e: e.tensor_tensor(out=out, in0=a, in1=b_, op=op), r, w)

        def TS(eng, out, a, s1, op0, r, w, s2=None, op1=None):
            if op1 is None:
                return P.op(eng, lambda e: e.tensor_scalar(out=out, in0=a, scalar1=s1, scalar2=None, op0=op0), r, w)
            return P.op(eng, lambda e: e.tensor_scalar(out=out, in0=a, scalar1=s1, scalar2=s2, op0=op0, op1=op1), r, w)

        def STT(out, in0, scalar, in1, op0, op1, r, w):
            return P.op(DVE, lambda e: e.scalar_tensor_tensor(out=out, in0=in0, scalar=scalar, in1=in1, op0=op0, op1=op1), r, w)

        def CP(eng, out, in_, r, w):
            return P.op(eng, lambda e: e.tensor_copy(out=out, in_=in_), r, w)

        def MS(eng, ap, val, w):
            return P.op(eng, lambda e: e.memset(ap, val), (), w)

        def DMA(eng, out, in_, r, w, semkey=None):
            return P.op(eng, lambda e: e.dma_start(out=out, in_=in_), r, w, dma=True, semkey=semkey)

        bank_rr = [0]

        def PB(i):
            return ps[:, i, :]

        MS(POOL, ident[:], 0.0, ["ident"])
        P.op(POOL, lambda e: e.affine_select(out=ident[:], in_=ident[:], pattern=[[-1, 128]], compare_op=ALU.not_equal,
                                             fill=1.0, base=0, channel_multiplier=1), ["ident"], ["ident"])
        CP(DVE, identb[:], ident[:], ["ident"], ["identb"])
        for f_ in range(2):
            CP(DVE, permb[:].rearrange("p (c f i) -> p c f i", c=2, f=2)[:, :, f_, :],
               ident[:].rearrange("p (c f i) -> p c f i", c=2, f=2)[:, :, 1 - f_, :], ["ident"], [("permb", f_)])
        MS(POOL, negh[:], -0.5, ["negh"])
        MS(POOL, CVT, 0.0, [("R2", "cvt")])
        DMA(SP, CVT[0:3, :], cv_d, (), [("R2", "cvt")])
        for kc in range(8):
            TR(ps[:, kc // 4, (kc % 4) * 128:(kc % 4 + 1) * 128], CVT[:, kc * 128:(kc + 1) * 128], ident[:],
               [("R2", "cvt"), "ident"], [("ps", kc // 4)])
        MS(POOL, scT[:], 1.0, ["scT"])
        for hb in range(2):
            ACTF(scT[:, hb * 4:(hb + 1) * 4, 0:3], ps[:, hb, :].rearrange("p (k n) -> p k n", k=4)[:, :, 0:3], AF.Silu,
                 [("ps", hb), "scT"], ["scT"])
        for n in range(2):
            CP(DVE, SCB[:, :, n, :], scT[:, :, n:n + 1].broadcast_to([128, 9, 128]), ["scT"], [("R2", "SCB", n)])
        for s_ in range(2):
            MS(POOL, WM[:, s_, 8, :], 0.0, [("R2", "wm", s_)])
        fine = lambda l_, cb_: ("mcol", l_, cb_)
        allfine = [fine(l_, cb_) for l_ in range(2) for cb_ in range(6)]
        MS(POOL, mcol[:], 0.0, ["mcol"] + allfine)
        blkc = [0]

        def mod_block(l, cb):
            s_ = blkc[0] % 2
            blkc[0] += 1
            blk = blkc[0]
            DMA(POOL, WM[:, s_, 0:8, :], wmod_d[l].rearrange("(k p) n -> p k n", p=128)[:, :, cb * 1024:(cb + 1) * 1024],
                (), [("R2", "wm", s_)])
            DMA(POOL, WM[0:1, s_, 8, :], bmod_d[l:l + 1, cb * 1024:(cb + 1) * 1024], (), [("R2", "wm", s_)],
                semkey=("R2", "wmb", s_))
            if cb not in (2, 5):
                bk = 6 + (blk % 2)
                for jj in range(8):
                    for kc in range(9):
                        MM(ps[:, bk, jj * 4:jj * 4 + 3], WM[:, s_, kc, jj * 128:(jj + 1) * 128], scT[:, kc, 0:3],
                           kc == 0, kc == 8, [("R2", "wm", s_), "scT"], [("ps", bk)])
                CP(DVE, mcol[:, l, cb * 8:cb * 8 + 8, 0:3], ps[:, bk, 0:32].rearrange("p (j n) -> p j n", j=8)[:, :, 0:3],
                   [("ps", bk), fine(l, cb)], [fine(l, cb)])
            else:
                which = 0 if cb == 2 else 1
                for half in range(2):
                    for n in range(2):
                        bk = 4 + n
                        for kc in range(9):
                            MM(PB(bk), SCB[:, kc, n, :], WM[:, s_, kc, half * 512:(half + 1) * 512], kc == 0, kc == 8,
                               [("R2", "wm", s_), ("R2", "SCB", n)], [("ps", bk)])
                        ACTF(GROW[:, n, :], PB(bk), AF.Identity, [("ps", bk)], [("R2", "grow", n)])
                        DMA(SP, gd_d[l, n, which:which + 1, half * 512:(half + 1) * 512], GROW[0:1, n, :],
                            [("R2", "grow", n)], [("gd", l, n, which, half)], semkey=("gdst", n))

        tcount_ = [0]

        def ht_tile(b, kind, i, mkeys):
            slot = tcount_[0] % 2
            tcount_[0] += 1
            srcd = ctx_d[b, i * 128:(i + 1) * 128, :] if kind == "c" else x_d[b, i * 128:(i + 1) * 128, :]
            DMA(SP, zt[:, slot, :], srcd, (), [("zt", slot)])
            for j in range(NCH):
                bk = 2 * slot + j // 4
                TR(ps[:, bk, (j % 4) * 128:(j % 4 + 1) * 128], zt[:, slot, j * 128:(j + 1) * 128], ident[:],
                   [("zt", slot), "ident"], [("ps", bk)])
            for j in range(NCH):
                bk = 2 * slot + j // 4
                if kind == "c":
                    dst_, dk_, n_ = hcT[:, j, i * 128:(i + 1) * 128], ("R2", "hcT", j), 2
                else:
                    dst_, dk_, n_ = hT[:, j, i * 128:(i + 1) * 128], ("hT", j, i // TPB), b
                src_ = ps[:, bk, (j % 4) * 128:(j % 4 + 1) * 128]
                if j < 4:
                    ACTF(dst_, src_, AF.Identity, [("ps", bk)] + mkeys, [dk_], scale=mcol[:, 0, 8 + j, n_:n_ + 1], bias=mcol[:, 0, j, n_:n_ + 1])
                else:
                    TS(DVE, dst_, src_, mcol[:, 0, 8 + j, n_:n_ + 1], ALU.mult, [("ps", bk)] + mkeys, [dk_],
                       s2=mcol[:, 0, j, n_:n_ + 1], op1=ALU.add)

        order = [(l_, cb_) for l_ in range(2) for cb_ in range(6) if not (l_ == 1 and cb_ == 0)]
        for (l_, cb_) in order[:2]:
            mod_block(l_, cb_)
        TS(DVE, mcol[:, 0, 8:16, :], mcol[:, 0, 8:16, :], 1.0, ALU.add, [fine(0, 1)], [fine(0, 1)])
        rest = order[2:]
        mk0 = [fine(0, 0), fine(0, 1)]
        for i in range(NT):
            ht_tile(0, "x", i, mk0)
            nblk = (len(rest) * (i + 1)) // NT - (len(rest) * i) // NT
            for _ in range(nblk):
                mod_block(*rest.pop(0))
        while rest:
            mod_block(*rest.pop(0))
        TS(DVE, mcol[:, 1, 8:16, :], mcol[:, 1, 8:16, :], 1.0, ALU.add, allfine + ["mcol"], ["mcol"] + allfine)
        for l in range(2):
            TS(DVE, mcol[:, l, 32:40, :], mcol[:, l, 32:40, :], 1.0, ALU.add, ["mcol"], ["mcol"])
            TS(DVE, sc2a[:, l, :, :], mcol[:, l, 32:40, :], 1.0 / ALPHA, ALU.mult, ["mcol"], ["sc2a"])
        lam_init = lam_inits[0]
        DMA(SP, LAMT.rearrange("p a b -> p (a b)"), lam_d.rearrange("a b -> (a b)").rearrange("(o n) -> o n", o=1).broadcast_to([128, 256]),
            (), [("R2", "lamt")])
        TT(DVE, LAMT[:, 0, :], LAMT[:, 0, :], LAMT[:, 1, :], ALU.mult, [("R2", "lamt")], [("R2", "lamt")])
        TT(DVE, LAMT[:, 2, :], LAMT[:, 2, :], LAMT[:, 3, :], ALU.mult, [("R2", "lamt")], [("R2", "lamt")])
        P.op(DVE, lambda e: e.tensor_reduce(out=small[:, 0:1], in_=LAMT[:, 0, :], axis=AX.X, op=ALU.add), [("R2", "lamt")], ["sm01"])
        P.op(DVE, lambda e: e.tensor_reduce(out=small[:, 1:2], in_=LAMT[:, 2, :], axis=AX.X, op=ALU.add), [("R2", "lamt"), "sm01"], ["sm01"])
        ACTF(small[:, 2:4], small[:, 0:2], AF.Exp, ["sm01"], ["sm23"])
        TT(DVE, small[:, 4:5], small[:, 3:4], small[:, 2:3], ALU.subtract, ["sm23"], ["sm4"])
        TS(DVE, small[:, 5:6], small[:, 4:5], -lam_init, ALU.add, ["sm4"], ["neglam"])
        DMA(SP, gsub[:], subg_d.broadcast_to([128, 128]), (), ["gsub"])
        TS(DVE, gsub[:], gsub[:], 1.0 - lam_init, ALU.mult, ["gsub"], ["gsub"])

        class LnPipe:
            def __init__(self, final_eng=DVE):
                self.final_eng = final_eng
                self.p1 = []
                self.p2 = []

            def _A(self, slot):
                z = zt[:, slot, :]
                key = ("zt", slot)
                for c2 in range(2):
                    P.op(DVE, lambda e, c2=c2: e.bn_stats(out=stats[:, slot, c2, :], in_=z[:, c2 * 512:(c2 + 1) * 512]),
                         [key], [("stats", slot, c2)])
                P.op(DVE, lambda e: e.bn_aggr(out=mv[:, slot, 0:2], in_=stats[:, slot, :, :].rearrange("p a b -> p (a b)")),
                     [("stats", slot, 0), ("stats", slot, 1)], [("mv", slot, 0)])
                TS(DVE, mv[:, slot, 2:3], mv[:, slot, 1:2], LN_EPS, ALU.add, [("mv", slot, 0)], [("mv", slot, 1)])
                TT(POOL, mv[:, slot, 2:3], mv[:, slot, 2:3], negh[:, 0:1], ALU.pow, [("mv", slot, 1), "negh"], [("mv", slot, 1)])

            def _B(self, slot):
                z = zt[:, slot, :]
                key = ("zt", slot)
                STT(mv[:, slot, 3:4], mv[:, slot, 0:1], -1.0, mv[:, slot, 2:3], ALU.mult, ALU.mult,
                    [("mv", slot, 0), ("mv", slot, 1)], [("mv", slot, 2)])
                ACTF(z, z, AF.Identity, [key, ("mv", slot, 1), ("mv", slot, 2)], [key],
                     scale=mv[:, slot, 2:3], bias=mv[:, slot, 3:4])
                TT(POOL, z, z, TLG[:], ALU.mult, [key, "TLG"], [key])

            def _C(self, item):
                slot, dst, dst_key, after = item
                TT(self.final_eng, dst, zt[:, slot, :], TLB[:], ALU.add, [("zt", slot), "TLB"], [dst_key])
                if after is not None:
                    after()

            def push(self, slot, dst, dst_key, after=None):
                self._A(slot)
                if self.p1:
                    it_ = self.p1.pop(0)
                    self._B(it_[0])
                    self.p2.append(it_)
                if len(self.p2) > 1 or (self.p2 and not self.p1 and False):
                    self._C(self.p2.pop(0))
                self.p1.append((slot, dst, dst_key, after))

            def flush(self):
                while self.p1 or self.p2:
                    if self.p1:
                        it_ = self.p1.pop(0)
                        self._B(it_[0])
                        self.p2.append(it_)
                    if self.p2 and (len(self.p2) > 1 or not self.p1):
                        self._C(self.p2.pop(0))

        def load_bc(tile, src_row, key, eng=SP, reads=()):
            return DMA(eng, tile[:], src_row.broadcast_to([128, D]), reads, [key])

        def gd_keys(l_, b_, which):
            return [("gd", l_, b_, which, 0), ("gd", l_, b_, which, 1)]

        def transpose_mod(srcv, src_keys, nt, dstT, dst_key, scale_col, bias_col, col_keys):
            nblk = (nt + 3) // 4
            cnt = 0
            for bq in range(nblk):
                tl = list(range(bq * 4, min(nt, bq * 4 + 4)))
                for j in range(NCH):
                    bk = cnt % 2
                    cnt += 1
                    for ti, t_ in enumerate(tl):
                        TR(ps[:, bk, ti * 128:(ti + 1) * 128], srcv[:, t_, j * 128:(j + 1) * 128], ident[:],
                           [src_keys(t_), "ident"], [("ps", bk)])
                    w_ = len(tl) * 128
                    ACTF(dstT[:, j, tl[0] * 128:tl[0] * 128 + w_], ps[:, bk, 0:w_], AF.Identity,
                         [("ps", bk)] + col_keys, [dst_key(j, bq)], scale=scale_col(j), bias=bias_col(j))

        out_stores = []
        for b in range(2):
            P.switch("XR")
            P.switch("R2")
            l = 0
            MS(POOL, KA[64:128, :], 0.0, [("XR", "KAz")])
            MS(POOL, KBm[0:64, :], 0.0, [("XR", "KBz")])
            MS(POOL, VX[:, :, 128:136], 1.0, [("XR", "VXo")])
            DMA(SP, COS, cos_d, (), [("XR", "cos")])
            DMA(SP, SIN, sin_d, (), [("XR", "sin")])
            DMA(POOL, WO, wo_d.rearrange("(k p) n -> p k n", p=128), (), [("R2", "wo")])
            for i in range(NCT):
                ht_tile(b, "c", i, ["mcol"])
            if b > 0:
                for i in range(NT):
                    ht_tile(b, "x", i, ["mcol"])
            wv = wqkv_d.rearrange("(k p) n -> p k n", p=128)
            def load_head(hh):
                sl = hh % 2
                for wi in range(3):
                    DMA(POOL, WH2[:, sl, wi], wv[:, :, wi * D + hh * 128: wi * D + (hh + 1) * 128], (), [("XR", "wh", sl, wi)])
            load_head(0)
            deferred = []
            git = [0]
            for h in range(H):
                hs = h % 2
                WH = WH2[:, hs]
                if h + 1 < H:
                    load_head(h + 1)
                blocks = [(wsel, bq) for wsel in range(2) for bq in range(NQB)]

                def rope_s1(n):
                    wsel, bq = blocks[n]
                    tsl = slice(bq * QB, (bq + 1) * QB)
                    b0_ = 0 if n % 2 == 0 else 2
                    for kc in range(8):
                        MM(ps[:, b0_, 0:QB], WH[:, wsel, kc, :], hT[:, kc, tsl], kc == 0, kc == 7, [("XR", "wh", hs, wsel), ("hT", kc, bq)], [("ps", b0_)])
                    ACTF(QBF[:, n % 2, :], ps[:, b0_, 0:QB], AF.Identity, [("ps", b0_)], [("XR", "qbf", n % 2)])
                    ACTF(QF[:, n % 2, :], ps[:, b0_, 0:QB], AF.Identity, [("ps", b0_)], [("XR", "qf", n % 2)])

                def rope_s2(n):
                    wsel, bq = blocks[n]
                    tsl = slice(bq * QB, (bq + 1) * QB)
                    ksl = slice(NCT * 128 + bq * QB, NCT * 128 + (bq + 1) * QB)
                    b0_, b1_ = (0, 1) if n % 2 == 0 else (2, 3)
                    MM(ps[:, b1_, 0:QB], permb[:], QBF[:, n % 2, :], True, True, [("permb", 0), ("permb", 1), ("XR", "qbf", n % 2)], [("ps", b1_)])
                    TT(POOL, RT[:, 0, :], QF[:, n % 2, :], COS[:, tsl], ALU.mult, [("XR", "qf", n % 2), ("XR", "cos")], [("rt", 0)])
                    TT(DVE, RT[:, 1, :], ps[:, b1_, 0:QB], SIN[:, tsl], ALU.mult, [("ps", b1_), ("XR", "sin")], [("rt", 1)])
                    if wsel == 0:
                        TT(DVE, QT[:, tsl], RT[:, 0, :], RT[:, 1, :], ALU.add, [("rt", 0), ("rt", 1)], [("XR", "qt", bq)])
                    else:
                        TT(DVE, KA[0:64, ksl], RT[0:64, 0, :], RT[0:64, 1, :], ALU.add, [("rt", 0), ("rt", 1)], [("XR", "ka", bq)])
                        TT(DVE, KBm[64:128, ksl], RT[64:128, 0, :], RT[64:128, 1, :], ALU.add, [("rt", 0), ("rt", 1)], [("XR", "kb", bq)])
                for n in range(len(blocks)):
                    rope_s1(n)
                    if n >= 1:
                        rope_s2(n - 1)
                rope_s2(len(blocks) - 1)
                for kc in range(8):
                    MM(ps[:, 0, 0:CTX], WH[:, 1, kc, :], hcT[:, kc, :], kc == 0, kc == 7, [("XR", "wh", hs, 1), ("R2", "hcT", kc)], [("ps", 0)])
                ACTF(KA[0:64, 0:CTX], ps[0:64, 0, 0:CTX], AF.Identity, [("ps", 0)], [("XR", "ka", "c")])
                ACTF(KBm[64:128, 0:CTX], ps[64:128, 0, 0:CTX], AF.Identity, [("ps", 0)], [("XR", "kb", "c")])
                for g0 in range(0, KT, 4):
                    tl = list(range(g0, min(KT, g0 + 4)))
                    bk = 1 + (g0 // 4) % 3
                    for ti, kt in enumerate(tl):
                        for kc in range(8):
                            if kt < NCT:
                                lt, rk = hcT[:, kc, kt * 128:(kt + 1) * 128], ("R2", "hcT", kc)
                            else:
                                tt = kt - NCT
                                lt, rk = hT[:, kc, tt * 128:(tt + 1) * 128], ("hT", kc, tt // TPB)
                            MM(ps[:, bk, ti * 128:(ti + 1) * 128], lt, WH[:, 2, kc, :], kc == 0, kc == 7, [("XR", "wh", hs, 2), rk], [("ps", bk)])
                    ACTF(VX[:, tl[0]:tl[0] + len(tl), 0:128], ps[:, bk, 0:len(tl) * 128].rearrange("p (t e) -> p t e", e=128), AF.Identity,
                         [("ps", bk)], [("XR", "vx", g0)])
                its = [(bq, kt) for bq in range(NQB) for kt in range(KT)]

                def scores(it):
                    bq, kt = its[it]
                    sset = it % 2
                    qsl = slice(bq * QB, (bq + 1) * QB)
                    kk = slice(kt * 128, (kt + 1) * 128)
                    krd_a = [("XR", "KAz"), ("XR", "ka", "c") if kt < NCT else ("XR", "ka", (kt - NCT) // TPB)]
                    krd_b = [("XR", "KBz"), ("XR", "kb", "c") if kt < NCT else ("XR", "kb", (kt - NCT) // TPB)]
                    MM(ps[:, 2 * sset, 0:QB], KA[:, kk], QT[:, qsl], True, True, krd_a + [("XR", "qt", bq)], [("ps", 2 * sset)])
                    MM(ps[:, 2 * sset + 1, 0:QB], KBm[:, kk], QT[:, qsl], True, True, krd_b + [("XR", "qt", bq)], [("ps", 2 * sset + 1)])

                def finish_block(h, bq):
                    qsl = slice(bq * QB, (bq + 1) * QB)
                    psb = ps[:, 7, :].bitcast(BF16)
                    for j in range(TPB):
                        TR(psb[:, j * 128:(j + 1) * 128], onb[:, j, :], identb[:], [("onb", j), "identb"], [("ps", 7)])
                    ACTF(onT[:, h, qsl], psb[:, 0:QB], AF.Identity, [("ps", 7)], [("R2", "onT", h, bq)])

                scores(0)
                for it in range(len(its)):
                    bq, kt = its[it]
                    sset = it % 2
                    if it + 1 < len(its):
                        scores(it + 1)
                    git[0] += 1
                    if deferred and deferred[0][0] <= git[0]:
                        finish_block(*deferred.pop(0)[1])
                    ACTF(EB[:, sset], ps[:, 2 * sset:2 * sset + 2, 0:QB], AF.Exp, [("ps", 2 * sset), ("ps", 2 * sset + 1)],
                         [("XR", "eb", sset)], scale=0.125)
                    for c in range(2):
                        for j in range(TPB):
                            a = c * TPB + j
                            bkk = 4 + a // 3
                            o_ = (a % 3) * 160
                            first_in_bank = (a % 3 == 0)
                            MM(ps[:, bkk, o_:o_ + 129], EB[:, sset, c, j * 128:(j + 1) * 128], VX[:, kt, 0:129],
                               (kt == 0) and first_in_bank, kt == KT - 1,
                               [("XR", "eb", sset), ("XR", "vx", (kt // 4) * 4), ("XR", "VXo")], [("ps", bkk)], skip=True)
                    if kt != KT - 1:
                        continue
                    nb = (NACC + 2) // 3
                    for bb in range(nb):
                        na = min(3, NACC - 3 * bb)
                        CP(DVE, accS[:, bb, 0:na * 130].rearrange("p (a e) -> p a e", e=130)[:, :, 0:129],
                           ps[:, 4 + bb, 0:na * 160].rearrange("p (a e) -> p a e", e=160)[:, :, 0:129], [("ps", 4 + bb)], [("accS", bb)])
                    accf = accS[:].rearrange("p b n -> p (b n)")

                    def AV_(a):
                        base = (a // 3) * 390 + (a % 3) * 130
                        return accf[:, base:base + 128], accf[:, base + 128:base + 129]
                    akeys = [("accS", bb) for bb in range(nb)]
                    for a in range(NACC):
                        P.op(DVE, lambda e, a=a: e.reciprocal(out=small[:, 8 + a:9 + a], in_=AV_(a)[1]), akeys, [("rr", a)])
                    for j in range(TPB):
                        TT(DVE, small[:, 8 + TPB + j:9 + TPB + j], small[:, 8 + TPB + j:9 + TPB + j], small[:, 5:6], ALU.mult,
                           [("rr", TPB + j), "neglam"], [("rr", TPB + j)])
                    for j in range(TPB):
                        TS(DVE, ework[:, 0, j * 128:(j + 1) * 128], AV_(TPB + j)[0], small[:, 8 + TPB + j:9 + TPB + j], ALU.mult,
                           akeys + [("rr", TPB + j)], [("ew", 0, j)])
                    for j in range(TPB):
                        STT(ework[:, 1, j * 128:(j + 1) * 128], AV_(j)[0], small[:, 8 + j:9 + j], ework[:, 0, j * 128:(j + 1) * 128],
                            ALU.mult, ALU.add, akeys + [("rr", j), ("ew", 0, j)], [("ew", 1, j)])
                    for j in range(TPB):
                        TT(DVE, ework[:, 2, j * 128:(j + 1) * 128], ework[:, 1, j * 128:(j + 1) * 128], ework[:, 1, j * 128:(j + 1) * 128],
                           ALU.mult, [("ew", 1, j)], [("ew", 2, j)])
                    P.op(DVE, lambda e: e.tensor_reduce(out=small[:, 24:24 + TPB], in_=ework[:, 2, 0:TPB * 128].rearrange("p (j e) -> p j e", e=128),
                                                        axis=AX.X, op=ALU.add), [("ew", 2, j) for j in range(TPB)], ["ss"])
                    TS(DVE, small[:, 24:24 + TPB], small[:, 24:24 + TPB], 1.0 / 128.0, ALU.mult, ["ss"], ["ss"], s2=LN_EPS, op1=ALU.add)
                    TT(POOL, small[:, 28:28 + TPB], small[:, 24:24 + TPB], negh[:, 0:TPB], ALU.pow, ["ss", "negh"], ["rstd"])
                    while deferred:
                        finish_block(*deferred.pop(0)[1])
                    for j in range(TPB):
                        STT(onb[:, j, :], ework[:, 1, j * 128:(j + 1) * 128], small[:, 28 + j:29 + j], gsub[:], ALU.mult, ALU.mult,
                            [("ew", 1, j), "rstd", "gsub"], [("onb", j)])
                    deferred.append((git[0] + 12, (h, bq)))
            while deferred:
                finish_block(*deferred.pop(0)[1])
            P.switch("XR")
            load_bc(TG, gd_d[0, b, 0:1, :], "TG", reads=gd_keys(0, b, 0))
            load_bc(TLG, lng_d[0, 0:1, :], "TLG"); load_bc(TLB, lnb_d[0, 0:1, :], "TLB")
            TS(DVE, TLG[:], TLG[:], ALPHA, ALU.mult, ["TLG"], ["TLG"])
            TS(DVE, TLB[:], TLB[:], ALPHA, ALU.mult, ["TLB"], ["TLB"])
            lnp = LnPipe()
            for i in range(NT):
                slot = i % 3
                bq = i // TPB
                DMA(SP, XRv[:, i, :], x_d[b, i * 128:(i + 1) * 128, :], (), [("XR", "x", i)])
                bks = (0, 1) if i % 2 == 0 else (2, 3)
                for hf in range(2):
                    for h in range(H):
                        MM(ps[:, bks[hf], :], onT[:, h, i * 128:(i + 1) * 128], WO[:, h, hf * 512:(hf + 1) * 512], h == 0, h == H - 1,
                           [("R2", "onT", h, bq), ("R2", "wo")], [("ps", bks[hf])])
                z = zt[:, slot, :]
                TT(DVE, z, ps[:, bks[0]:bks[0] + 2, :].rearrange("p a n -> p (a n)"), TG[:], ALU.mult,
                   [("ps", bks[0]), ("ps", bks[1]), "TG"], [("zt", slot)])
                STT(z, XRv[:, i, :], ALPHA, z, ALU.mult, ALU.add, [("XR", "x", i), ("zt", slot)], [("zt", slot)])
                lnp.push(slot, XRv[:, i, :], ("XR", "x", i))
            lnp.flush()

            for l in range(2):
                if l == 1:
                    P.switch("R2")
                    DMA(SP, PWF, poolw_d.rearrange("g (c p) n -> p g c n", p=128), (), [("R2", "pwf")])
                    DMA(POOL, BAND, band_d.rearrange("g v p n -> p g v n"), (), [("R2", "band")])
                    DMA(SP, INVE.rearrange("p g e n -> p (g e n)"),
                        inve_d.rearrange("g e n -> (g e n)").rearrange("(o n) -> o n", o=1).broadcast_to([128, 8 * 128]), (), [("R2", "inve")])
                    load_bc(TG, gd_d[1, b, 0:1, :], "TG", reads=gd_keys(1, b, 0))
                    DMA(SP, zt[:, 0, :], pscale_d.broadcast_to([128, D]), (), [("zt", 0)])
                    TT(DVE, TG[:], TG[:], zt[:, 0, :], ALU.mult, ["TG", ("zt", 0)], ["TG"])
                    for ci in range(2):
                        TT(DVE, PW[:, :, ci, :], PWF[:, :, ci, :], TG[:].rearrange("p (g o) -> p g o", g=4), ALU.mult,
                           [("R2", "pwf"), "TG"], [("R2", "pw", ci)])
                    load_bc(TLG, lng_d[1, 0:1, :], "TLG"); load_bc(TLB, lnb_d[1, 0:1, :], "TLB")
                    TS(DVE, TLG[:], TLG[:], ALPHA, ALU.mult, ["TLG"], ["TLG"])
                    TS(DVE, TLB[:], TLB[:], ALPHA, ALU.mult, ["TLB"], ["TLB"])

                    def split_tile(i):
                        s4 = i % 4
                        ACTF(XH[:, s4, 0, :], XRv[:, i, :], AF.Identity, [("XR", "x", i)], [("R2", "xh", s4, 0)])
                        TT(DVE, XH[:, s4, 1, :], XRv[:, i, :], XH[:, s4, 0, :], ALU.subtract, [("XR", "x", i), ("R2", "xh", s4, 0)], [("R2", "xh", s4, 1)])
                    split_tile(0)
                    if NT > 1:
                        split_tile(1)
                    pcnt = 0
                    lnp = LnPipe(POOL)
                    def pooled_tile(i):
                        nonlocal pcnt
                        for ch in range(NCH):
                            g = ch // 2
                            bk = 4 + (pcnt % 4)
                            pcnt += 1
                            srcs = [s_ for s_ in (i - 1, i, i + 1) if 0 <= s_ < NT]
                            nmm = len(srcs) * 2
                            m_ = 0
                            for s_ in srcs:
                                rel = s_ - i
                                if rel == 0:
                                    v = 3 if i == 0 else (4 if i == NT - 1 else 1)
                                else:
                                    v = 0 if rel == -1 else 2
                                for a_ in range(2):
                                    MM(ps[:, bk, 0:128], XH[:, s_ % 4, a_, ch * 128:(ch + 1) * 128], BAND[:, g, v, :], m_ == 0, m_ == nmm - 1,
                                       [("R2", "xh", s_ % 4, a_), ("R2", "band")], [("ps", bk)])
                                    m_ += 1
                            if i == 0 or i == NT - 1:
                                STT(hT[:, ch, i * 128:(i + 1) * 128], ps[:, bk, 0:128], mcol[:, 1, 8 + ch, b:b + 1], INVE[:, g, 0 if i == 0 else 1, :],
                                    ALU.mult, ALU.mult, [("ps", bk), "mcol", ("R2", "inve")], [("hTt", ch, i), ("hT", ch, i // TPB)])
                            else:
                                TS(DVE, hT[:, ch, i * 128:(i + 1) * 128], ps[:, bk, 0:128], mcol[:, 1, 8 + ch, b:b + 1], ALU.mult,
                                   [("ps", bk), "mcol"], [("hTt", ch, i), ("hT", ch, i // TPB)], s2=1.0 / WINDOWS[g], op1=ALU.mult)

                    def mix_tile(i):
                        slot = i % 3
                        bks = (0, 1) if i % 2 == 0 else (2, 3)
                        for g in range(4):
                            for ci in range(2):
                                MM(ps[:, bks[g // 2], (g % 2) * 256:(g % 2 + 1) * 256], hT[:, 2 * g + ci, i * 128:(i + 1) * 128], PW[:, g, ci, :],
                                   ci == 0, ci == 1, [("hTt", 2 * g + ci, i), ("R2", "pw", ci)], [("ps", bks[g // 2])])
                        STT(zt[:, slot, :], XRv[:, i, :], ALPHA, ps[:, bks[0]:bks[0] + 2, :].rearrange("p a n -> p (a n)"), ALU.mult, ALU.add,
                            [("XR", "x", i), ("ps", bks[0]), ("ps", bks[1])], [("zt", slot)])
                        lnp.push(slot, XRv[:, i, :], ("XR", "x", i))

                    for i in range(NT + 1):
                        if i < NT:
                            pooled_tile(i)
                            if i + 2 < NT:
                                split_tile(i + 2)
                        if i >= 1:
                            mix_tile(i - 1)
                    lnp.flush()
                P.switch("R2")
                load_bc(TG, gd_d[l, b, 1:2, :], "TG", reads=gd_keys(l, b, 1))
                load_bc(TLG, lng_d[l, 1:2, :], "TLG"); load_bc(TLB, lnb_d[l, 1:2, :], "TLB")
                cnt = 0
                for bq in range(NQB):
                    for j in range(NCH):
                        bk = (cnt // 2) % 2 + (0 if j % 2 == 0 else 2)
                        cnt += 1
                        for ti in range(TPB):
                            t_ = bq * TPB + ti
                            TR(ps[:, bk, ti * 128:(ti + 1) * 128], XRv[:, t_, j * 128:(j + 1) * 128], ident[:], [("XR", "x", t_), "ident"], [("ps", bk)])
                        hk = [("hT", j, bq)] + ([("hTt", j, bq * TPB + ti) for ti in range(TPB)] if l == 1 else [])
                        if j % 2 == 0:
                            ACTF(hT[:, j, bq * QB:(bq + 1) * QB], ps[:, bk, 0:QB], AF.Identity, [("ps", bk), "mcol", "sc2a"], hk,
                                 scale=sc2a[:, l, j, b:b + 1], bias=mcol[:, l, 24 + j, b:b + 1])
                        else:
                            TS(DVE, hT[:, j, bq * QB:(bq + 1) * QB], ps[:, bk, 0:QB], sc2a[:, l, j, b:b + 1], ALU.mult,
                               [("ps", bk), "mcol", "sc2a"], hk, s2=mcol[:, l, 24 + j, b:b + 1], op1=ALU.add)
                winv = win_d[l].rearrange("(k p) n -> p k n", p=128)
                gi = 0
                lnp2 = LnPipe()
                for grp in groups:
                    s_ = gi % 2
                    gi += 1
                    ng = len(grp)
                    c0 = grp[0]
                    DMA(POOL, WIN[:, s_, 0, :, 0:ng * 128], winv[:, :, c0 * 128:(c0 + ng) * 128], (), [("R2", "win", s_, 0)])
                    DMA(POOL, WIN[:, s_, 1, :, 0:ng * 128], winv[:, :, DFF + c0 * 128:DFF + (c0 + ng) * 128], (), [("R2", "win", s_, 1)])
                    for ci, c in enumerate(grp):
                        ws = (gi * GC + ci) % 2
                        DMA(SP, WST[:, ws, :], wout_d[l, c * 128:(c + 1) * 128, :], (), [("R2", "wst", ws)])
                        TT(DVE, WOUT[:, s_, ci, :], WST[:, ws, :], TG[:], ALU.mult, [("R2", "wst", ws), "TG"], [("R2", "wout", s_, ci)])
                    for bq in range(NQB):
                        tsl = slice(bq * QB, (bq + 1) * QB)
                        as_ = bq % 2
                        for ci, c in enumerate(grp):
                            sgs = ci % 2
                            bg, bu = (0, 1) if (ci % 2 == 0) else (2, 3)
                            for kc in range(8):
                                MM(ps[:, bg, 0:QB], WIN[:, s_, 0, kc, ci * 128:(ci + 1) * 128], hT[:, kc, tsl], kc == 0, kc == 7,
                                   [("R2", "win", s_, 0), ("hT", kc, bq)], [("ps", bg)])
                            for kc in range(8):
                                MM(ps[:, bu, 0:QB], WIN[:, s_, 1, kc, ci * 128:(ci + 1) * 128], hT[:, kc, tsl], kc == 0, kc == 7,
                                   [("R2", "win", s_, 1), ("hT", kc, bq)], [("ps", bu)])
                            ACTF(SG[:, sgs, :], ps[:, bg, 0:QB], AF.Silu, [("ps", bg)], [("R2", "sg", sgs)])
                            TT(DVE, AT[:, as_, ci, :], ps[:, bu, 0:QB], SG[:, sgs, :], ALU.mult, [("ps", bu), ("R2", "sg", sgs)], [("R2", "at", as_, ci)])
                        for ti in range(TPB):
                            t_ = bq * TPB + ti
                            bo = (4, 5) if (ti % 2 == 0) else (6, 7)
                            for hf in range(2):
                                for ci in range(ng):
                                    MM(ps[:, bo[hf], :], AT[:, as_, ci, ti * 128:(ti + 1) * 128], WOUT[:, s_, ci, hf * 512:(hf + 1) * 512],
                                       ci == 0, ci == ng - 1, [("R2", "at", as_, ci), ("R2", "wout", s_, ci)], [("ps", bo[hf])])
                            if grp is not groups[-1]:
                                TT(DVE, XRv[:, t_, :], ps[:, bo[0]:bo[0] + 2, :].rearrange("p a n -> p (a n)"), XRv[:, t_, :], ALU.add,
                                   [("ps", bo[0]), ("ps", bo[1]), ("XR", "x", t_)], [("XR", "x", t_)])
                            else:
                                slot = t_ % 3
                                TT(DVE, zt[:, slot, :], ps[:, bo[0]:bo[0] + 2, :].rearrange("p a n -> p (a n)"), XRv[:, t_, :], ALU.add,
                                   [("ps", bo[0]), ("ps", bo[1]), ("XR", "x", t_)], [("zt", slot)])
                                def store_(t_=t_, b=b):
                                    out_stores.append(DMA(SP, out_d[b, t_ * 128:(t_ + 1) * 128, :], XRv[:, t_, :], [("XR", "x", t_)],
                                                          [("out", b, t_)], semkey=("ost", t_ % 4)))
                                lnp2.push(slot, XRv[:, t_, :], ("XR", "x", t_), store_ if l == 1 else None)
                lnp2.flush()
        P.emit(final_waits=out_stores[-8:])
    return nc


_CACHE = {}


def make_core_inputs(inp, core, S, CTX):
    f = lambda a: np.ascontiguousarray(np.asarray(a, dtype=np.float32))
    b0 = 2 * core
    cos128, sin128 = rope_tables(S)
    band, invedge = pool_tables(S)
    return {
        "x": f(inp["x"][b0:b0 + 2]),
        "cvec": f(np.concatenate([np.asarray(inp["c"])[b0:b0 + 2], np.asarray(inp["c_ctx"])[None, :]], axis=0)),
        "ctx": f(inp["ctx"][b0:b0 + 2]),
        "w_mod": f(inp["w_mod"]), "b_mod": f(inp["b_mod"]),
        "ln_g": f(inp["ln_g"]), "ln_b": f(inp["ln_b"]),
        "w_qkv": f(inp["attn_w_qkv"][0]), "lam": f(inp["attn_lambda"][0]),
        "subg": f(np.asarray(inp["attn_subln_g"])[0][None, :]),
        "w_o": f(inp["attn_w_o"][0]), "pool_w": f(inp["pool_w"][0]),
        "pool_scale": f(np.asarray(inp["pool_scale"])[0][None, :]),
        "w_in": f(inp["ffn_w_in"]), "w_out": f(inp["ffn_w_out"]),
        "rope_cos": cos128, "rope_sin": sin128, "band": band, "invedge": invedge,
    }


def kernel(**inputs):
    S = inputs["x"].shape[1]
    CTX = inputs["ctx"].shape[1]
    DFF = inputs["ffn_w_out"].shape[1]
    B = inputs["x"].shape[0]
    ncores = B // 2
    key = (S, CTX, DFF)
    if key not in _CACHE:
        _CACHE[key] = build(S, CTX, DFF)
    nc = _CACHE[key]
    in_maps = [make_core_inputs(inputs, c, S, CTX) for c in range(ncores)]
    res = run_bass_kernel_spmd(nc, in_maps, core_ids=list(range(ncores)))
    out = np.concatenate([np.asarray(r["out"]) for r in res.results], axis=0)
    return out.astype(np.float32)
```

```python
import math
import contextlib
import numpy as np
import concourse.bass as bass
import concourse.mybir as mybir
from concourse.bass_utils import run_bass_kernel_spmd

F32 = mybir.dt.float32
BF16 = mybir.dt.bfloat16
AF = mybir.ActivationFunctionType
ALU = mybir.AluOpType
AX = mybir.AxisListType
PE, ACT, DVE, POOL, SP = "tensor", "scalar", "vector", "gpsimd", "sync"
ENGS = (PE, ACT, DVE, POOL, SP)

D = 1024
NCH = 8
H = 8
ALPHA = (2.0 * 2) ** 0.25
LN_EPS = 1e-5
WINDOWS = (2, 4, 8, 16)
GRID_W = 64


class Rec:
    __slots__ = ("eng", "fn", "deps", "sig", "count", "dma", "semkey", "cum", "idx")


class Prog:
    def __init__(self, nc):
        self.nc = nc
        self.recs = {e: [] for e in ENGS}
        self.last_w = {}
        self.readers = {}
        self.dma_cum = {}
        self.n = 0
        self.pending = {}
        self.seen = {}

    def switch(self, region):
        lat = {}
        dmas = []
        old = self.pending.get(region, [])
        cand = list(old)
        for k in list(self.last_w.keys()):
            if isinstance(k, tuple) and k[0] == region:
                cand.append(self.last_w.pop(k))
        for k in list(self.readers.keys()):
            if isinstance(k, tuple) and k[0] == region:
                cand.extend(self.readers.pop(k))
        for r in cand:
            if r.dma:
                dmas.append(r)
            else:
                if r.eng not in lat or lat[r.eng].idx < r.idx:
                    lat[r.eng] = r
        self.pending[region] = list(lat.values()) + dmas
        self.seen[region] = set()

    def op(self, eng, fn, reads=(), writes=(), dma=False, semkey=None):
        r = Rec()
        r.eng, r.fn, r.dma, r.sig, r.count, r.idx = eng, fn, dma, False, 0, self.n
        self.n += 1
        deps = set()
        for k in tuple(reads) + tuple(writes):
            if isinstance(k, tuple) and k[0] in self.pending and k not in self.seen[k[0]]:
                self.seen[k[0]].add(k)
                deps.update(self.pending[k[0]])
        for k in reads:
            w = self.last_w.get(k)
            if w is not None:
                deps.add(w)
        for k in writes:
            w = self.last_w.get(k)
            if w is not None:
                deps.add(w)
            deps.update(self.readers.get(k, ()))
        for k in reads:
            self.readers.setdefault(k, []).append(r)
        for k in writes:
            self.last_w[k] = r
            self.readers[k] = []
        deps.discard(r)
        r.deps = deps
        if dma:
            r.semkey = semkey if semkey is not None else writes[0]
            self.dma_cum[r.semkey] = self.dma_cum.get(r.semkey, 0) + 16
            r.cum = self.dma_cum[r.semkey]
        else:
            r.semkey, r.cum = None, 0
        self.recs[eng].append(r)
        return r

    def emit(self, final_waits=()):
        nc = self.nc
        for e in ENGS:
            for r in self.recs[e]:
                for d in r.deps:
                    if d.dma:
                        continue
                    if d.eng == PE and r.eng == PE and not r.dma:
                        continue
                    d.sig = True
        for r in final_waits:
            if not r.dma:
                r.sig = True
        for e in ENGS:
            c = 0
            for r in self.recs[e]:
                if r.dma:
                    continue
                if r.sig:
                    c += 1
                    r.count = c
        with contextlib.ExitStack() as st:
            esem = {e: st.enter_context(nc.semaphore("p_" + e)) for e in ENGS}
            dsem = {}
            for i, k in enumerate(self.dma_cum):
                dsem[k] = st.enter_context(nc.semaphore("d%d" % i))
            block = st.enter_context(nc.Block())

            def make(e):
                def body(eng):
                    waited = {}

                    def wait(sem, val, key):
                        if waited.get(key, 0) >= val:
                            return
                        waited[key] = val
                        eng.wait_ge(sem, val)

                    for r in self.recs[e]:
                        tg = {}
                        for d in r.deps:
                            if d.dma:
                                k_ = ("d", d.semkey)
                                if tg.get(k_, (None, 0))[1] < d.cum:
                                    tg[k_] = (dsem[d.semkey], d.cum)
                            else:
                                if d.eng == PE and e == PE and not r.dma:
                                    continue
                                k_ = ("e", d.eng)
                                if tg.get(k_, (None, 0))[1] < d.count:
                                    tg[k_] = (esem[d.eng], d.count)
                        for k_ in sorted(tg, key=str):
                            wait(tg[k_][0], tg[k_][1], k_)
                        ins = r.fn(eng)
                        if r.dma:
                            ins.then_inc(dsem[r.semkey], 16)
                        elif r.sig:
                            ins.then_inc(esem[e], 1)
                    if e == SP:
                        tg = {}
                        for r in final_waits:
                            if r.dma:
                                k_ = ("d", r.semkey)
                                if tg.get(k_, (None, 0))[1] < r.cum:
                                    tg[k_] = (dsem[r.semkey], r.cum)
                            else:
                                k_ = ("e", r.eng)
                                if tg.get(k_, (None, 0))[1] < r.count:
                                    tg[k_] = (esem[r.eng], r.count)
                        for k_ in sorted(tg, key=str):
                            wait(tg[k_][0], tg[k_][1], k_)
                return body

            for e in ENGS:
                getattr(block, e)(make(e))


def rope_tables(S):
    rows = S // GRID_W
    t = np.arange(S)
    row = (t // GRID_W).astype(np.float32)
    col = (t % GRID_W).astype(np.float32)
    inv = np.power(np.float32(10000.0), -np.arange(16, dtype=np.float32) / np.float32(16)).astype(np.float32)
    ang = np.concatenate([row[:, None] * inv, col[:, None] * inv], axis=-1).astype(np.float32)
    cos = np.cos(ang).astype(np.float32).T
    sin = np.sin(ang).astype(np.float32).T
    cos128 = np.tile(cos, (4, 1))
    sin128 = np.concatenate([-sin, sin, -sin, sin], axis=0)
    return np.ascontiguousarray(cos128), np.ascontiguousarray(sin128)


def pool_tables(S):
    NT = S // 128
    band = np.zeros((4, 5, 128, 128), np.float32)
    invedge = np.zeros((4, 2, 128), np.float32)

    def full(w):
        t = np.arange(S)
        lo = np.clip(t - w // 2, 0, S)
        hi = np.clip(t + w - w // 2, 0, S)
        return lo, hi

    for g, w in enumerate(WINDOWS):
        lo, hi = full(w)
        cnt = (hi - lo)

        def blockmat(ti, si):
            m = np.zeros((128, 128), np.float32)
            for tt in range(128):
                t = ti * 128 + tt
                for tp in range(lo[t], hi[t]):
                    if si * 128 <= tp < (si + 1) * 128:
                        m[tp - si * 128, tt] += 1.0
                if si == ti:
                    m[tt, tt] -= cnt[t]
            return m
        mid = 1 if NT > 2 else 0
        band[g, 0] = blockmat(mid, mid - 1) if mid >= 1 else 0
        band[g, 1] = blockmat(mid, mid) if NT > 2 else 0
        band[g, 2] = blockmat(mid, mid + 1) if mid + 1 < NT else 0
        band[g, 3] = blockmat(0, 0)
        band[g, 4] = blockmat(NT - 1, NT - 1)
        if NT <= 2:
            band[g, 0] = blockmat(1, 0)
            band[g, 2] = blockmat(0, 1)
        invedge[g, 0] = 1.0 / cnt[0:128]
        invedge[g, 1] = 1.0 / cnt[S - 128:S]
    return band, invedge


def build(S=2048, CTX=256, DFF=2816):
    NT = S // 128
    NCT = CTX // 128
    KT = NCT + NT
    QB = min(512, S)
    NQB = S // QB
    TPB = QB // 128
    NFC = DFF // 128
    GC = 4
    groups = [list(range(a, min(a + GC, NFC))) for a in range(0, NFC, GC)]
    NACC = 2 * TPB
    lam_inits = [0.8 - 0.6 * math.exp(-0.3 * 0)]

    nc = bass.Bass("TRN2", target_bir_lowering=False)
    dr = lambda n, s, k="ExternalInput": nc.dram_tensor(n, s, F32, kind=k).ap()
    x_d = dr("x", [2, S, D]); cv_d = dr("cvec", [3, D]); ctx_d = dr("ctx", [2, CTX, D])
    wmod_d = dr("w_mod", [2, D, 6 * D]); bmod_d = dr("b_mod", [2, 6 * D])
    lng_d = dr("ln_g", [2, 2, D]); lnb_d = dr("ln_b", [2, 2, D])
    wqkv_d = dr("w_qkv", [D, 3 * D]); lam_d = dr("lam", [4, 64]); subg_d = dr("subg", [1, 128])
    wo_d = dr("w_o", [D, D]); poolw_d = dr("pool_w", [4, 256, 256]); pscale_d = dr("pool_scale", [1, D])
    win_d = dr("w_in", [2, D, 2 * DFF]); wout_d = dr("w_out", [2, DFF, D])
    cos_d = dr("rope_cos", [128, S]); sin_d = dr("rope_sin", [128, S])
    band_d = dr("band", [4, 5, 128, 128]); inve_d = dr("invedge", [4, 2, 128])
    out_d = dr("out", [2, S, D], "ExternalOutput")
    gd_d = dr("gscratch", [2, 2, 2, D], "Internal")

    st = contextlib.ExitStack()
    with st:
        sb = lambda n, s, d: st.enter_context(nc.sbuf_tensor(n, s, d))
        XR = sb("XR", [128, 16 * 1024], F32)
        hT = sb("hT", [128, NCH, S], BF16)
        R2 = sb("R2", [128, 17 * 1024], F32)
        TG = sb("TG", [128, D], F32); TLG = sb("TLG", [128, D], F32)
        TLB = sb("TLB", [128, D], F32)
        ident = sb("ident", [128, 128], F32)
        identb = sb("identb", [128, 128], BF16)
        permb = sb("permb", [128, 128], BF16)
        mcol = sb("mcol", [128, 2, 48, 4], F32)
        sc2a = sb("sc2a", [128, 2, 8, 4], F32)
        scT = sb("scT", [128, 9, 4], BF16)
        small = sb("small", [128, 64], F32)
        stats = sb("stats", [128, 3, 2, 6], F32)
        mv = sb("mv", [128, 3, 4], F32)
        gsub = sb("gsub", [128, 128], F32)
        ework = sb("ework", [128, 3, 4 * 128], F32)
        accS = sb("accS", [128, 3, 390], F32)
        onb = sb("onb", [128, 4, 128], BF16)
        zt = sb("zt", [128, 3, D], F32)
        negh = sb("negh", [128, 8], F32)
        RT = sb("RT", [128, 2, QB], F32)
        ps = st.enter_context(nc.psum_tensor("ps", [128, 8, 512], F32))
        P = Prog(nc)

        XRv = XR[:].rearrange("p (t d) -> p t d", d=D) if NT == 16 else XR[:, 0:NT * D].rearrange("p (t d) -> p t d", d=D)
        XRb = XR[:].bitcast(BF16)
        XRf = XR[:]
        off = [0]

        def carve_b(n):
            a = XRb[:, off[0]:off[0] + n]
            off[0] += n
            return a
        QT = carve_b(S)
        KA = carve_b(KT * 128); KBm = carve_b(KT * 128)
        VX = carve_b(KT * 136).rearrange("p (k e) -> p k e", e=136)
        WH2 = carve_b(2 * 3 * 1024).rearrange("p (s w k n) -> p s w k n", s=2, w=3, k=8)
        QBF = carve_b(2 * QB).rearrange("p (s q) -> p s q", s=2)
        QF = carve_b(4 * QB).bitcast(F32).rearrange("p (s q) -> p s q", s=2)
        EB = carve_b(2 * 2 * QB).rearrange("p (s c q) -> p s c q", s=2, c=2)
        assert off[0] % 2 == 0
        foff = off[0] // 2
        COS = XRf[:, foff:foff + S]; SIN = XRf[:, foff + S:foff + 2 * S]
        foff += 2 * S
        assert foff <= 16 * 1024, foff

        R2b = R2[:].bitcast(BF16)
        R2f = R2[:]
        onT = R2b[:, 0:NCH * S].rearrange("p (h s) -> p h s", h=NCH)
        WO = R2b[:, 16384:16384 + 8 * D].rearrange("p (k n) -> p k n", k=8)
        hcT = R2b[:, 24576:24576 + NCH * CTX].rearrange("p (c s) -> p c s", c=NCH)
        LAMT = R2f[:, 17024:17024 + 256].rearrange("p (a b) -> p a b", a=4)
        CVT = R2f[:, 0:1024]
        WM = R2b[:, 2048:2048 + 3 * 9 * 1024].rearrange("p (s k n) -> p s k n", s=3, k=9)
        GROW = R2f[:, 14848:14848 + 2 * 512].rearrange("p (s n) -> p s n", s=2)
        SCB = R2b[:, 31744:31744 + 9 * 2 * 128].rearrange("p (k n m) -> p k n m", k=9, n=2)
        WIN = R2b[:, 0:2 * 2 * 8 * GC * 128].rearrange("p (s u k n) -> p s u k n", s=2, u=2, k=8)
        o2 = 2 * 2 * 8 * GC * 128
        WOUT = R2b[:, o2:o2 + 2 * GC * D].rearrange("p (s c n) -> p s c n", s=2, c=GC)
        o2 += 2 * GC * D
        AT = R2b[:, o2:o2 + 2 * GC * QB].rearrange("p (s c q) -> p s c q", s=2, c=GC)
        o2 += 2 * GC * QB
        assert o2 % 2 == 0
        f2 = o2 // 2
        WST = R2f[:, f2:f2 + 2 * D].rearrange("p (s n) -> p s n", s=2)
        f2 += 2 * D
        SG = R2f[:, f2:f2 + 2 * QB].rearrange("p (s q) -> p s q", s=2)
        f2 += 2 * QB
        assert f2 <= 17 * 1024, f2
        XH = R2b[:, 0:4 * 2 * D].rearrange("p (s a d) -> p s a d", s=4, a=2)
        PW = R2b[:, 8192:8192 + 4 * 2 * 256].rearrange("p (g c n) -> p g c n", g=4, c=2)
        BAND = R2b[:, 10240:10240 + 20 * 128].rearrange("p (g v n) -> p g v n", g=4, v=5)
        INVE = R2f[:, 6400:6400 + 8 * 128].rearrange("p (g e n) -> p g e n", g=4, e=2)
        PWF = R2f[:, 7424:7424 + 4 * 2 * 256].rearrange("p (g c n) -> p g c n", g=4, c=2)

        def MM(out, lhsT, rhs, start, stop, r, w, skip=False):
            return P.op(PE, lambda e: e.matmul(out, lhsT=lhsT, rhs=rhs, start=start, stop=stop, skip_group_check=skip), r, w)

        def TR(out, in_, idn, r, w):
            return P.op(PE, lambda e: e.transpose(out, in_, idn), r, w)

        def ACTF(out, in_, func, r, w, scale=1.0, bias=0.0):
            return P.op(ACT, lambda e: e.activation(out=out, in_=in_, func=func, scale=scale, bias=bias), r, w)

        def TT(eng, out, a, b_, op, r, w):
            return P.op(eng, lambda e: e.tensor_tensor(out=out, in0=a, in1=b_, op=op), r, w)

        def TS(eng, out, a, s1, op0, r, w, s2=None, op1=None):
            if op1 is None:
                return P.op(eng, lambda e: e.tensor_scalar(out=out, in0=a, scalar1=s1, scalar2=None, op0=op0), r, w)
            return P.op(eng, lambda e: e.tensor_scalar(out=out, in0=a, scalar1=s1, scalar2=s2, op0=op0, op1=op1), r, w)

        def STT(out, in0, scalar, in1, op0, op1, r, w):
            return P.op(DVE, lambda e: e.scalar_tensor_tensor(out=out, in0=in0, scalar=scalar, in1=in1, op0=op0, op1=op1), r, w)

        def CP(eng, out, in_, r, w):
            return P.op(eng, lambda e: e.tensor_copy(out=out, in_=in_), r, w)

        def MS(eng, ap, val, w):
            return P.op(eng, lambda e: e.memset(ap, val), (), w)

        def DMA(eng, out, in_, r, w, semkey=None):
            return P.op(eng, lambda e: e.dma_start(out=out, in_=in_), r, w, dma=True, semkey=semkey)

        bank_rr = [0]

        def PB(i):
            return ps[:, i, :]

        MS(POOL, ident[:], 0.0, ["ident"])
        P.op(POOL, lambda e: e.affine_select(out=ident[:], in_=ident[:], pattern=[[-1, 128]], compare_op=ALU.not_equal,
                                             fill=1.0, base=0, channel_multiplier=1), ["ident"], ["ident"])
        CP(DVE, identb[:], ident[:], ["ident"], ["identb"])
        for f_ in range(2):
            CP(DVE, permb[:].rearrange("p (c f i) -> p c f i", c=2, f=2)[:, :, f_, :],
               ident[:].rearrange("p (c f i) -> p c f i", c=2, f=2)[:, :, 1 - f_, :], ["ident"], [("permb", f_)])
        MS(POOL, negh[:], -0.5, ["negh"])
        MS(POOL, CVT, 0.0, [("R2", "cvt")])
        DMA(SP, CVT[0:3, :], cv_d, (), [("R2", "cvt")])
        for kc in range(8):
            TR(ps[:, kc // 4, (kc % 4) * 128:(kc % 4 + 1) * 128], CVT[:, kc * 128:(kc + 1) * 128], ident[:],
               [("R2", "cvt"), "ident"], [("ps", kc // 4)])
        MS(POOL, scT[:], 1.0, ["scT"])
        for hb in range(2):
            ACTF(scT[:, hb * 4:(hb + 1) * 4, 0:3], ps[:, hb, :].rearrange("p (k n) -> p k n", k=4)[:, :, 0:3], AF.Silu,
                 [("ps", hb), "scT"], ["scT"])
        for n in range(2):
            CP(DVE, SCB[:, :, n, :], scT[:, :, n:n + 1].broadcast_to([128, 9, 128]), ["scT"], [("R2", "SCB", n)])
        for s_ in range(3):
            MS(POOL, WM[:, s_, 8, :], 0.0, [("R2", "wm", s_)])
        fine = lambda l_, cb_: ("mcol", l_, cb_)
        allfine = [fine(l_, cb_) for l_ in range(2) for cb_ in range(6)]
        MS(POOL, mcol[:], 0.0, ["mcol"] + allfine)
        blkc = [0]

        def mod_block(l, cb):
            s_ = blkc[0] % 3
            blkc[0] += 1
            blk = blkc[0]
            DMA(POOL, WM[:, s_, 0:8, :], wmod_d[l].rearrange("(k p) n -> p k n", p=128)[:, :, cb * 1024:(cb + 1) * 1024],
                (), [("R2", "wm", s_)])
            DMA(POOL, WM[0:1, s_, 8, :], bmod_d[l:l + 1, cb * 1024:(cb + 1) * 1024], (), [("R2", "wm", s_)],
                semkey=("R2", "wmb", s_))
            if cb not in (2, 5):
                bk = 6 + (blk % 2)
                for jj in range(8):
                    for kc in range(9):
                        MM(ps[:, bk, jj * 4:jj * 4 + 3], WM[:, s_, kc, jj * 128:(jj + 1) * 128], scT[:, kc, 0:3],
                           kc == 0, kc == 8, [("R2", "wm", s_), "scT"], [("ps", bk)])
                CP(DVE, mcol[:, l, cb * 8:cb * 8 + 8, 0:3], ps[:, bk, 0:32].rearrange("p (j n) -> p j n", j=8)[:, :, 0:3],
                   [("ps", bk), fine(l, cb)], [fine(l, cb)])
            else:
                which = 0 if cb == 2 else 1
                for half in range(2):
                    for n in range(2):
                        bk = 4 + n
                        for kc in range(9):
                            MM(PB(bk), SCB[:, kc, n, :], WM[:, s_, kc, half * 512:(half + 1) * 512], kc == 0, kc == 8,
                               [("R2", "wm", s_), ("R2", "SCB", n)], [("ps", bk)])
                        ACTF(GROW[:, n, :], PB(bk), AF.Identity, [("ps", bk)], [("R2", "grow", n)])
                        DMA(SP, gd_d[l, n, which:which + 1, half * 512:(half + 1) * 512], GROW[0:1, n, :],
                            [("R2", "grow", n)], [("gd", l, n, which, half)], semkey=("gdst", n))

        tcount_ = [0]

        def ht_tile(b, kind, i, mkeys):
            slot = tcount_[0] % 2
            tcount_[0] += 1
            srcd = ctx_d[b, i * 128:(i + 1) * 128, :] if kind == "c" else x_d[b, i * 128:(i + 1) * 128, :]
            DMA(SP, zt[:, slot, :], srcd, (), [("zt", slot)])
            for j in range(NCH):
                bk = 2 * slot + j // 4
                TR(ps[:, bk, (j % 4) * 128:(j % 4 + 1) * 128], zt[:, slot, j * 128:(j + 1) * 128], ident[:],
                   [("zt", slot), "ident"], [("ps", bk)])
            for j in range(NCH):
                bk = 2 * slot + j // 4
                if kind == "c":
                    dst_, dk_, n_ = hcT[:, j, i * 128:(i + 1) * 128], ("R2", "hcT", j), 2
                else:
                    dst_, dk_, n_ = hT[:, j, i * 128:(i + 1) * 128], ("hT", j, i // TPB), b
                src_ = ps[:, bk, (j % 4) * 128:(j % 4 + 1) * 128]
                if j < 4:
                    ACTF(dst_, src_, AF.Identity, [("ps", bk)] + mkeys, [dk_], scale=mcol[:, 0, 8 + j, n_:n_ + 1], bias=mcol[:, 0, j, n_:n_ + 1])
                else:
                    TS(DVE, dst_, src_, mcol[:, 0, 8 + j, n_:n_ + 1], ALU.mult, [("ps", bk)] + mkeys, [dk_],
                       s2=mcol[:, 0, j, n_:n_ + 1], op1=ALU.add)

        order = [(l_, cb_) for l_ in range(2) for cb_ in range(6) if not (l_ == 1 and cb_ == 0)]
        for (l_, cb_) in order[:2]:
            mod_block(l_, cb_)
        TS(DVE, mcol[:, 0, 8:16, :], mcol[:, 0, 8:16, :], 1.0, ALU.add, [fine(0, 1)], [fine(0, 1)])
        rest = order[2:]
        mk0 = [fine(0, 0), fine(0, 1)]
        for i in range(NT):
            ht_tile(0, "x", i, mk0)
            nblk = (len(rest) * (i + 1)) // NT - (len(rest) * i) // NT
            for _ in range(nblk):
                mod_block(*rest.pop(0))
        while rest:
            mod_block(*rest.pop(0))
        TS(DVE, mcol[:, 1, 8:16, :], mcol[:, 1, 8:16, :], 1.0, ALU.add, allfine + ["mcol"], ["mcol"] + allfine)
        for l in range(2):
            TS(DVE, mcol[:, l, 32:40, :], mcol[:, l, 32:40, :], 1.0, ALU.add, ["mcol"], ["mcol"])
            TS(DVE, sc2a[:, l, :, :], mcol[:, l, 32:40, :], 1.0 / ALPHA, ALU.mult, ["mcol"], ["sc2a"])
        lam_init = lam_inits[0]
        DMA(SP, LAMT.rearrange("p a b -> p (a b)"), lam_d.rearrange("a b -> (a b)").rearrange("(o n) -> o n", o=1).broadcast_to([128, 256]),
            (), [("R2", "lamt")])
        TT(DVE, LAMT[:, 0, :], LAMT[:, 0, :], LAMT[:, 1, :], ALU.mult, [("R2", "lamt")], [("R2", "lamt")])
        TT(DVE, LAMT[:, 2, :], LAMT[:, 2, :], LAMT[:, 3, :], ALU.mult, [("R2", "lamt")], [("R2", "lamt")])
        P.op(DVE, lambda e: e.tensor_reduce(out=small[:, 0:1], in_=LAMT[:, 0, :], axis=AX.X, op=ALU.add), [("R2", "lamt")], ["sm01"])
        P.op(DVE, lambda e: e.tensor_reduce(out=small[:, 1:2], in_=LAMT[:, 2, :], axis=AX.X, op=ALU.add), [("R2", "lamt"), "sm01"], ["sm01"])
        ACTF(small[:, 2:4], small[:, 0:2], AF.Exp, ["sm01"], ["sm23"])
        TT(DVE, small[:, 4:5], small[:, 3:4], small[:, 2:3], ALU.subtract, ["sm23"], ["sm4"])
        TS(DVE, small[:, 5:6], small[:, 4:5], -lam_init, ALU.add, ["sm4"], ["neglam"])
        DMA(SP, gsub[:], subg_d.broadcast_to([128, 128]), (), ["gsub"])
        TS(DVE, gsub[:], gsub[:], 1.0 - lam_init, ALU.mult, ["gsub"], ["gsub"])

        class LnPipe:
            def __init__(self, final_eng=DVE):
                self.final_eng = final_eng
                self.p1 = []
                self.p2 = []

            def _A(self, slot):
                z = zt[:, slot, :]
                key = ("zt", slot)
                for c2 in range(2):
                    P.op(DVE, lambda e, c2=c2: e.bn_stats(out=stats[:, slot, c2, :], in_=z[:, c2 * 512:(c2 + 1) * 512]),
                         [key], [("stats", slot, c2)])
                P.op(DVE, lambda e: e.bn_aggr(out=mv[:, slot, 0:2], in_=stats[:, slot, :, :].rearrange("p a b -> p (a b)")),
                     [("stats", slot, 0), ("stats", slot, 1)], [("mv", slot, 0)])
                TS(DVE, mv[:, slot, 2:3], mv[:, slot, 1:2], LN_EPS, ALU.add, [("mv", slot, 0)], [("mv", slot, 1)])
                TT(POOL, mv[:, slot, 2:3], mv[:, slot, 2:3], negh[:, 0:1], ALU.pow, [("mv", slot, 1), "negh"], [("mv", slot, 1)])

            def _B(self, slot):
                z = zt[:, slot, :]
                key = ("zt", slot)
                STT(mv[:, slot, 3:4], mv[:, slot, 0:1], -1.0, mv[:, slot, 2:3], ALU.mult, ALU.mult,
                    [("mv", slot, 0), ("mv", slot, 1)], [("mv", slot, 2)])
                ACTF(z, z, AF.Identity, [key, ("mv", slot, 1), ("mv", slot, 2)], [key],
                     scale=mv[:, slot, 2:3], bias=mv[:, slot, 3:4])
                TT(POOL, z, z, TLG[:], ALU.mult, [key, "TLG"], [key])

            def _C(self, item):
                slot, dst, dst_key, after = item
                TT(self.final_eng, dst, zt[:, slot, :], TLB[:], ALU.add, [("zt", slot), "TLB"], [dst_key])
                if after is not None:
                    after()

            def push(self, slot, dst, dst_key, after=None):
                self._A(slot)
                if self.p1:
                    it_ = self.p1.pop(0)
                    self._B(it_[0])
                    self.p2.append(it_)
                if len(self.p2) > 1 or (self.p2 and not self.p1 and False):
                    self._C(self.p2.pop(0))
                self.p1.append((slot, dst, dst_key, after))

            def flush(self):
                while self.p1 or self.p2:
                    if self.p1:
                        it_ = self.p1.pop(0)
                        self._B(it_[0])
                        self.p2.append(it_)
                    if self.p2 and (len(self.p2) > 1 or not self.p1):
                        self._C(self.p2.pop(0))

        def load_bc(tile, src_row, key, eng=SP, reads=()):
            return DMA(eng, tile[:], src_row.broadcast_to([128, D]), reads, [key])

        def gd_keys(l_, b_, which):
            return [("gd", l_, b_, which, 0), ("gd", l_, b_, which, 1)]

        def transpose_mod(srcv, src_keys, nt, dstT, dst_key, scale_col, bias_col, col_keys):
            nblk = (nt + 3) // 4
            cnt = 0
            for bq in range(nblk):
                tl = list(range(bq * 4, min(nt, bq * 4 + 4)))
                for j in range(NCH):
                    bk = cnt % 2
                    cnt += 1
                    for ti, t_ in enumerate(tl):
                        TR(ps[:, bk, ti * 128:(ti + 1) * 128], srcv[:, t_, j * 128:(j + 1) * 128], ident[:],
                           [src_keys(t_), "ident"], [("ps", bk)])
                    w_ = len(tl) * 128
                    ACTF(dstT[:, j, tl[0] * 128:tl[0] * 128 + w_], ps[:, bk, 0:w_], AF.Identity,
                         [("ps", bk)] + col_keys, [dst_key(j, bq)], scale=scale_col(j), bias=bias_col(j))

        out_stores = []
        for b in range(2):
            P.switch("XR")
            P.switch("R2")
            l = 0
            MS(POOL, KA[64:128, :], 0.0, [("XR", "KAz")])
            MS(POOL, KBm[0:64, :], 0.0, [("XR", "KBz")])
            MS(POOL, VX[:, :, 128:136], 1.0, [("XR", "VXo")])
            DMA(SP, COS, cos_d, (), [("XR", "cos")])
            DMA(SP, SIN, sin_d, (), [("XR", "sin")])
            DMA(POOL, WO, wo_d.rearrange("(k p) n -> p k n", p=128), (), [("R2", "wo")])
            for i in range(NCT):
                ht_tile(b, "c", i, ["mcol"])
            if b > 0:
                for i in range(NT):
                    ht_tile(b, "x", i, ["mcol"])
            wv = wqkv_d.rearrange("(k p) n -> p k n", p=128)
            def load_head(hh):
                sl = hh % 2
                for wi in range(3):
                    DMA(POOL, WH2[:, sl, wi], wv[:, :, wi * D + hh * 128: wi * D + (hh + 1) * 128], (), [("XR", "wh", sl, wi)])
            load_head(0)
            deferred = []
            git = [0]
            for h in range(H):
                hs = h % 2
                WH = WH2[:, hs]
                if h + 1 < H:
                    load_head(h + 1)
                blocks = [(wsel, bq) for wsel in range(2) for bq in range(NQB)]

                def rope_s1(n):
                    wsel, bq = blocks[n]
                    tsl = slice(bq * QB, (bq + 1) * QB)
                    b0_ = 0 if n % 2 == 0 else 2
                    for kc in range(8):
                        MM(ps[:, b0_, 0:QB], WH[:, wsel, kc, :], hT[:, kc, tsl], kc == 0, kc == 7, [("XR", "wh", hs, wsel), ("hT", kc, bq)], [("ps", b0_)])
                    ACTF(QBF[:, n % 2, :], ps[:, b0_, 0:QB], AF.Identity, [("ps", b0_)], [("XR", "qbf", n % 2)])
                    ACTF(QF[:, n % 2, :], ps[:, b0_, 0:QB], AF.Identity, [("ps", b0_)], [("XR", "qf", n % 2)])

                def rope_s2(n):
                    wsel, bq = blocks[n]
                    tsl = slice(bq * QB, (bq + 1) * QB)
                    ksl = slice(NCT * 128 + bq * QB, NCT * 128 + (bq + 1) * QB)
                    b0_, b1_ = (0, 1) if n % 2 == 0 else (2, 3)
                    MM(ps[:, b1_, 0:QB], permb[:], QBF[:, n % 2, :], True, True, [("permb", 0), ("permb", 1), ("XR", "qbf", n % 2)], [("ps", b1_)])
                    TT(POOL, RT[:, 0, :], QF[:, n % 2, :], COS[:, tsl], ALU.mult, [("XR", "qf", n % 2), ("XR", "cos")], [("rt", 0)])
                    TT(DVE, RT[:, 1, :], ps[:, b1_, 0:QB], SIN[:, tsl], ALU.mult, [("ps", b1_), ("XR", "sin")], [("rt", 1)])
                    if wsel == 0:
                        TT(DVE, QT[:, tsl], RT[:, 0, :], RT[:, 1, :], ALU.add, [("rt", 0), ("rt", 1)], [("XR", "qt", bq)])
                    else:
                        TT(DVE, KA[0:64, ksl], RT[0:64, 0, :], RT[0:64, 1, :], ALU.add, [("rt", 0), ("rt", 1)], [("XR", "ka", bq)])
                        TT(DVE, KBm[64:128, ksl], RT[64:128, 0, :], RT[64:128, 1, :], ALU.add, [("rt", 0), ("rt", 1)], [("XR", "kb", bq)])
                for n in range(len(blocks)):
                    rope_s1(n)
                    if n >= 1:
                        rope_s2(n - 1)
                rope_s2(len(blocks) - 1)
                for kc in range(8):
                    MM(ps[:, 0, 0:CTX], WH[:, 1, kc, :], hcT[:, kc, :], kc == 0, kc == 7, [("XR", "wh", hs, 1), ("R2", "hcT", kc)], [("ps", 0)])
                ACTF(KA[0:64, 0:CTX], ps[0:64, 0, 0:CTX], AF.Identity, [("ps", 0)], [("XR", "ka", "c")])
                ACTF(KBm[64:128, 0:CTX], ps[64:128, 0, 0:CTX], AF.Identity, [("ps", 0)], [("XR", "kb", "c")])
                for g0 in range(0, KT, 4):
                    tl = list(range(g0, min(KT, g0 + 4)))
                    bk = 1 + (g0 // 4) % 3
                    for ti, kt in enumerate(tl):
                        for kc in range(8):
                            if kt < NCT:
                                lt, rk = hcT[:, kc, kt * 128:(kt + 1) * 128], ("R2", "hcT", kc)
                            else:
                                tt = kt - NCT
                                lt, rk = hT[:, kc, tt * 128:(tt + 1) * 128], ("hT", kc, tt // TPB)
                            MM(ps[:, bk, ti * 128:(ti + 1) * 128], lt, WH[:, 2, kc, :], kc == 0, kc == 7, [("XR", "wh", hs, 2), rk], [("ps", bk)])
                    ACTF(VX[:, tl[0]:tl[0] + len(tl), 0:128], ps[:, bk, 0:len(tl) * 128].rearrange("p (t e) -> p t e", e=128), AF.Identity,
                         [("ps", bk)], [("XR", "vx", g0)])
                its = [(bq, kt) for bq in range(NQB) for kt in range(KT)]

                def scores(it):
                    bq, kt = its[it]
                    sset = it % 2
                    qsl = slice(bq * QB, (bq + 1) * QB)
                    kk = slice(kt * 128, (kt + 1) * 128)
                    krd_a = [("XR", "KAz"), ("XR", "ka", "c") if kt < NCT else ("XR", "ka", (kt - NCT) // TPB)]
                    krd_b = [("XR", "KBz"), ("XR", "kb", "c") if kt < NCT else ("XR", "kb", (kt - NCT) // TPB)]
                    MM(ps[:, 2 * sset, 0:QB], KA[:, kk], QT[:, qsl], True, True, krd_a + [("XR", "qt", bq)], [("ps", 2 * sset)])
                    MM(ps[:, 2 * sset + 1, 0:QB], KBm[:, kk], QT[:, qsl], True, True, krd_b + [("XR", "qt", bq)], [("ps", 2 * sset + 1)])

                def finish_block(h, bq):
                    qsl = slice(bq * QB, (bq + 1) * QB)
                    psb = ps[:, 7, :].bitcast(BF16)
                    for j in range(TPB):
                        TR(psb[:, j * 128:(j + 1) * 128], onb[:, j, :], identb[:], [("onb", j), "identb"], [("ps", 7)])
                    ACTF(onT[:, h, qsl], psb[:, 0:QB], AF.Identity, [("ps", 7)], [("R2", "onT", h, bq)])

                scores(0)
                for it in range(len(its)):
                    bq, kt = its[it]
                    sset = it % 2
                    if it + 1 < len(its):
                        scores(it + 1)
                    git[0] += 1
                    if deferred and deferred[0][0] <= git[0]:
                        finish_block(*deferred.pop(0)[1])
                    ACTF(EB[:, sset], ps[:, 2 * sset:2 * sset + 2, 0:QB], AF.Exp, [("ps", 2 * sset), ("ps", 2 * sset + 1)],
                         [("XR", "eb", sset)], scale=0.125)
                    for c in range(2):
                        for j in range(TPB):
                            a = c * TPB + j
                            bkk = 4 + a // 3
                            o_ = (a % 3) * 160
                            first_in_bank = (a % 3 == 0)
                            MM(ps[:, bkk, o_:o_ + 129], EB[:, sset, c, j * 128:(j + 1) * 128], VX[:, kt, 0:129],
                               (kt == 0) and first_in_bank, kt == KT - 1,
                               [("XR", "eb", sset), ("XR", "vx", (kt // 4) * 4), ("XR", "VXo")], [("ps", bkk)], skip=True)
                    if kt != KT - 1:
                        continue
                    nb = (NACC + 2) // 3
                    for bb in range(nb):
                        na = min(3, NACC - 3 * bb)
                        CP(DVE, accS[:, bb, 0:na * 130].rearrange("p (a e) -> p a e", e=130)[:, :, 0:129],
                           ps[:, 4 + bb, 0:na * 160].rearrange("p (a e) -> p a e", e=160)[:, :, 0:129], [("ps", 4 + bb)], [("accS", bb)])
                    accf = accS[:].rearrange("p b n -> p (b n)")

                    def AV_(a):
                        base = (a // 3) * 390 + (a % 3) * 130
                        return accf[:, base:base + 128], accf[:, base + 128:base + 129]
                    akeys = [("accS", bb) for bb in range(nb)]
                    for a in range(NACC):
                        P.op(DVE, lambda e, a=a: e.reciprocal(out=small[:, 8 + a:9 + a], in_=AV_(a)[1]), akeys, [("rr", a)])
                    for j in range(TPB):
                        TT(DVE, small[:, 8 + TPB + j:9 + TPB + j], small[:, 8 + TPB + j:9 + TPB + j], small[:, 5:6], ALU.mult,
                           [("rr", TPB + j), "neglam"], [("rr", TPB + j)])
                    for j in range(TPB):
                        TS(DVE, ework[:, 0, j * 128:(j + 1) * 128], AV_(TPB + j)[0], small[:, 8 + TPB + j:9 + TPB + j], ALU.mult,
                           akeys + [("rr", TPB + j)], [("ew", 0, j)])
                    for j in range(TPB):
                        STT(ework[:, 1, j * 128:(j + 1) * 128], AV_(j)[0], small[:, 8 + j:9 + j], ework[:, 0, j * 128:(j + 1) * 128],
                            ALU.mult, ALU.add, akeys + [("rr", j), ("ew", 0, j)], [("ew", 1, j)])
                    for j in range(TPB):
                        TT(DVE, ework[:, 2, j * 128:(j + 1) * 128], ework[:, 1, j * 128:(j + 1) * 128], ework[:, 1, j * 128:(j + 1) * 128],
                           ALU.mult, [("ew", 1, j)], [("ew", 2, j)])
                    P.op(DVE, lambda e: e.tensor_reduce(out=small[:, 24:24 + TPB], in_=ework[:, 2, 0:TPB * 128].rearrange("p (j e) -> p j e", e=128),
                                                        axis=AX.X, op=ALU.add), [("ew", 2, j) for j in range(TPB)], ["ss"])
                    TS(DVE, small[:, 24:24 + TPB], small[:, 24:24 + TPB], 1.0 / 128.0, ALU.mult, ["ss"], ["ss"], s2=LN_EPS, op1=ALU.add)
                    TT(POOL, small[:, 28:28 + TPB], small[:, 24:24 + TPB], negh[:, 0:TPB], ALU.pow, ["ss", "negh"], ["rstd"])
                    while deferred:
                        finish_block(*deferred.pop(0)[1])
                    for j in range(TPB):
                        STT(onb[:, j, :], ework[:, 1, j * 128:(j + 1) * 128], small[:, 28 + j:29 + j], gsub[:], ALU.mult, ALU.mult,
                            [("ew", 1, j), "rstd", "gsub"], [("onb", j)])
                    deferred.append((git[0] + 12, (h, bq)))
            while deferred:
                finish_block(*deferred.pop(0)[1])
            P.switch("XR")
            load_bc(TG, gd_d[0, b, 0:1, :], "TG", reads=gd_keys(0, b, 0))
            load_bc(TLG, lng_d[0, 0:1, :], "TLG"); load_bc(TLB, lnb_d[0, 0:1, :], "TLB")
            TS(DVE, TLG[:], TLG[:], ALPHA, ALU.mult, ["TLG"], ["TLG"])
            TS(DVE, TLB[:], TLB[:], ALPHA, ALU.mult, ["TLB"], ["TLB"])
            lnp = LnPipe()
            for i in range(NT):
                slot = i % 3
                bq = i // TPB
                DMA(SP, XRv[:, i, :], x_d[b, i * 128:(i + 1) * 128, :], (), [("XR", "x", i)])
                bks = (0, 1) if i % 2 == 0 else (2, 3)
                for hf in range(2):
                    for h in range(H):
                        MM(ps[:, bks[hf], :], onT[:, h, i * 128:(i + 1) * 128], WO[:, h, hf * 512:(hf + 1) * 512], h == 0, h == H - 1,
                           [("R2", "onT", h, bq), ("R2", "wo")], [("ps", bks[hf])])
                z = zt[:, slot, :]
                TT(DVE, z, ps[:, bks[0]:bks[0] + 2, :].rearrange("p a n -> p (a n)"), TG[:], ALU.mult,
                   [("ps", bks[0]), ("ps", bks[1]), "TG"], [("zt", slot)])
                STT(z, XRv[:, i, :], ALPHA, z, ALU.mult, ALU.add, [("XR", "x", i), ("zt", slot)], [("zt", slot)])
                lnp.push(slot, XRv[:, i, :], ("XR", "x", i))
            lnp.flush()

            for l in range(2):
                if l == 1:
                    P.switch("R2")
                    DMA(SP, PWF, poolw_d.rearrange("g (c p) n -> p g c n", p=128), (), [("R2", "pwf")])
                    DMA(POOL, BAND, band_d.rearrange("g v p n -> p g v n"), (), [("R2", "band")])
                    DMA(SP, INVE.rearrange("p g e n -> p (g e n)"),
                        inve_d.rearrange("g e n -> (g e n)").rearrange("(o n) -> o n", o=1).broadcast_to([128, 8 * 128]), (), [("R2", "inve")])
                    load_bc(TG, gd_d[1, b, 0:1, :], "TG", reads=gd_keys(1, b, 0))
                    DMA(SP, zt[:, 0, :], pscale_d.broadcast_to([128, D]), (), [("zt", 0)])
                    TT(DVE, TG[:], TG[:], zt[:, 0, :], ALU.mult, ["TG", ("zt", 0)], ["TG"])
                    for ci in range(2):
                        TT(DVE, PW[:, :, ci, :], PWF[:, :, ci, :], TG[:].rearrange("p (g o) -> p g o", g=4), ALU.mult,
                           [("R2", "pwf"), "TG"], [("R2", "pw", ci)])
                    load_bc(TLG, lng_d[1, 0:1, :], "TLG"); load_bc(TLB, lnb_d[1, 0:1, :], "TLB")
                    TS(DVE, TLG[:], TLG[:], ALPHA, ALU.mult, ["TLG"], ["TLG"])
                    TS(DVE, TLB[:], TLB[:], ALPHA, ALU.mult, ["TLB"], ["TLB"])

                    def split_tile(i):
                        s4 = i % 4
                        ACTF(XH[:, s4, 0, :], XRv[:, i, :], AF.Identity, [("XR", "x", i)], [("R2", "xh", s4, 0)])
                        TT(DVE, XH[:, s4, 1, :], XRv[:, i, :], XH[:, s4, 0, :], ALU.subtract, [("XR", "x", i), ("R2", "xh", s4, 0)], [("R2", "xh", s4, 1)])
                    split_tile(0)
                    if NT > 1:
                        split_tile(1)
                    pcnt = 0
                    lnp = LnPipe(POOL)
                    def pooled_tile(i):
                        nonlocal pcnt
                        for ch in range(NCH):
                            g = ch // 2
                            bk = 4 + (pcnt % 4)
                            pcnt += 1
                            srcs = [s_ for s_ in (i - 1, i, i + 1) if 0 <= s_ < NT]
                            nmm = len(srcs) * 2
                            m_ = 0
                            for s_ in srcs:
                                rel = s_ - i
                                if rel == 0:
                                    v = 3 if i == 0 else (4 if i == NT - 1 else 1)
                                else:
                                    v = 0 if rel == -1 else 2
                                for a_ in range(2):
                                    MM(ps[:, bk, 0:128], XH[:, s_ % 4, a_, ch * 128:(ch + 1) * 128], BAND[:, g, v, :], m_ == 0, m_ == nmm - 1,
                                       [("R2", "xh", s_ % 4, a_), ("R2", "band")], [("ps", bk)])
                                    m_ += 1
                            if i == 0 or i == NT - 1:
                                STT(hT[:, ch, i * 128:(i + 1) * 128], ps[:, bk, 0:128], mcol[:, 1, 8 + ch, b:b + 1], INVE[:, g, 0 if i == 0 else 1, :],
                                    ALU.mult, ALU.mult, [("ps", bk), "mcol", ("R2", "inve")], [("hTt", ch, i), ("hT", ch, i // TPB)])
                            else:
                                TS(DVE, hT[:, ch, i * 128:(i + 1) * 128], ps[:, bk, 0:128], mcol[:, 1, 8 + ch, b:b + 1], ALU.mult,
                                   [("ps", bk), "mcol"], [("hTt", ch, i), ("hT", ch, i // TPB)], s2=1.0 / WINDOWS[g], op1=ALU.mult)

                    def mix_tile(i):
                        slot = i % 3
                        bks = (0, 1) if i % 2 == 0 else (2, 3)
                        for g in range(4):
                            for ci in range(2):
                                MM(ps[:, bks[g // 2], (g % 2) * 256:(g % 2 + 1) * 256], hT[:, 2 * g + ci, i * 128:(i + 1) * 128], PW[:, g, ci, :],
                                   ci == 0, ci == 1, [("hTt", 2 * g + ci, i), ("R2", "pw", ci)], [("ps", bks[g // 2])])
                        STT(zt[:, slot, :], XRv[:, i, :], ALPHA, ps[:, bks[0]:bks[0] + 2, :].rearrange("p a n -> p (a n)"), ALU.mult, ALU.add,
                            [("XR", "x", i), ("ps", bks[0]), ("ps", bks[1])], [("zt", slot)])
                        lnp.push(slot, XRv[:, i, :], ("XR", "x", i))

                    for i in range(NT + 1):
                        if i < NT:
                            pooled_tile(i)
                            if i + 2 < NT:
                                split_tile(i + 2)
                        if i >= 1:
                            mix_tile(i - 1)
                    lnp.flush()
                P.switch("R2")
                load_bc(TG, gd_d[l, b, 1:2, :], "TG", reads=gd_keys(l, b, 1))
                load_bc(TLG, lng_d[l, 1:2, :], "TLG"); load_bc(TLB, lnb_d[l, 1:2, :], "TLB")
                cnt = 0
                for bq in range(NQB):
                    for j in range(NCH):
                        bk = (cnt // 2) % 2 + (0 if j % 2 == 0 else 2)
                        cnt += 1
                        for ti in range(TPB):
                            t_ = bq * TPB + ti
                            TR(ps[:, bk, ti * 128:(ti + 1) * 128], XRv[:, t_, j * 128:(j + 1) * 128], ident[:], [("XR", "x", t_), "ident"], [("ps", bk)])
                        hk = [("hT", j, bq)] + ([("hTt", j, bq * TPB + ti) for ti in range(TPB)] if l == 1 else [])
                        if j % 2 == 0:
                            ACTF(hT[:, j, bq * QB:(bq + 1) * QB], ps[:, bk, 0:QB], AF.Identity, [("ps", bk), "mcol", "sc2a"], hk,
                                 scale=sc2a[:, l, j, b:b + 1], bias=mcol[:, l, 24 + j, b:b + 1])
                        else:
                            TS(DVE, hT[:, j, bq * QB:(bq + 1) * QB], ps[:, bk, 0:QB], sc2a[:, l, j, b:b + 1], ALU.mult,
                               [("ps", bk), "mcol", "sc2a"], hk, s2=mcol[:, l, 24 + j, b:b + 1], op1=ALU.add)
                winv = win_d[l].rearrange("(k p) n -> p k n", p=128)
                gi = 0
                lnp2 = LnPipe()
                for grp in groups:
                    s_ = gi % 2
                    gi += 1
                    ng = len(grp)
                    c0 = grp[0]
                    DMA(POOL, WIN[:, s_, 0, :, 0:ng * 128], winv[:, :, c0 * 128:(c0 + ng) * 128], (), [("R2", "win", s_, 0)])
                    DMA(POOL, WIN[:, s_, 1, :, 0:ng * 128], winv[:, :, DFF + c0 * 128:DFF + (c0 + ng) * 128], (), [("R2", "win", s_, 1)])
                    for ci, c in enumerate(grp):
                        ws = (gi * GC + ci) % 2
                        DMA(SP, WST[:, ws, :], wout_d[l, c * 128:(c + 1) * 128, :], (), [("R2", "wst", ws)])
                        TT(DVE, WOUT[:, s_, ci, :], WST[:, ws, :], TG[:], ALU.mult, [("R2", "wst", ws), "TG"], [("R2", "wout", s_, ci)])
                    for bq in range(NQB):
                        tsl = slice(bq * QB, (bq + 1) * QB)
                        as_ = bq % 2
                        for ci, c in enumerate(grp):
                            sgs = ci % 2
                            bg, bu = (0, 1) if (ci % 2 == 0) else (2, 3)
                            for kc in range(8):
                                MM(ps[:, bg, 0:QB], WIN[:, s_, 0, kc, ci * 128:(ci + 1) * 128], hT[:, kc, tsl], kc == 0, kc == 7,
                                   [("R2", "win", s_, 0), ("hT", kc, bq)], [("ps", bg)])
                            for kc in range(8):
                                MM(ps[:, bu, 0:QB], WIN[:, s_, 1, kc, ci * 128:(ci + 1) * 128], hT[:, kc, tsl], kc == 0, kc == 7,
                                   [("R2", "win", s_, 1), ("hT", kc, bq)], [("ps", bu)])
                            ACTF(SG[:, sgs, :], ps[:, bg, 0:QB], AF.Silu, [("ps", bg)], [("R2", "sg", sgs)])
                            TT(DVE, AT[:, as_, ci, :], ps[:, bu, 0:QB], SG[:, sgs, :], ALU.mult, [("ps", bu), ("R2", "sg", sgs)], [("R2", "at", as_, ci)])
                        for ti in range(TPB):
                            t_ = bq * TPB + ti
                            bo = (4, 5) if (ti % 2 == 0) else (6, 7)
                            for hf in range(2):
                                for ci in range(ng):
                                    MM(ps[:, bo[hf], :], AT[:, as_, ci, ti * 128:(ti + 1) * 128], WOUT[:, s_, ci, hf * 512:(hf + 1) * 512],
                                       ci == 0, ci == ng - 1, [("R2", "at", as_, ci), ("R2", "wout", s_, ci)], [("ps", bo[hf])])
                            if grp is not groups[-1]:
                                TT(DVE, XRv[:, t_, :], ps[:, bo[0]:bo[0] + 2, :].rearrange("p a n -> p (a n)"), XRv[:, t_, :], ALU.add,
                                   [("ps", bo[0]), ("ps", bo[1]), ("XR", "x", t_)], [("XR", "x", t_)])
                            else:
                                slot = t_ % 3
                                TT(DVE, zt[:, slot, :], ps[:, bo[0]:bo[0] + 2, :].rearrange("p a n -> p (a n)"), XRv[:, t_, :], ALU.add,
                                   [("ps", bo[0]), ("ps", bo[1]), ("XR", "x", t_)], [("zt", slot)])
                                def store_(t_=t_, b=b):
                                    out_stores.append(DMA(SP, out_d[b, t_ * 128:(t_ + 1) * 128, :], XRv[:, t_, :], [("XR", "x", t_)],
                                                          [("out", b, t_)], semkey=("ost", t_ % 4)))
                                lnp2.push(slot, XRv[:, t_, :], ("XR", "x", t_), store_ if l == 1 else None)
                lnp2.flush()
        P.emit(final_waits=out_stores[-8:])
    return nc


_CACHE = {}


def make_core_inputs(inp, core, S, CTX):
    f = lambda a: np.ascontiguousarray(np.asarray(a, dtype=np.float32))
    b0 = 2 * core
    cos128, sin128 = rope_tables(S)
    band, invedge = pool_tables(S)
    return {
        "x": f(inp["x"][b0:b0 + 2]),
        "cvec": f(np.concatenate([np.asarray(inp["c"])[b0:b0 + 2], np.asarray(inp["c_ctx"])[None, :]], axis=0)),
        "ctx": f(inp["ctx"][b0:b0 + 2]),
        "w_mod": f(inp["w_mod"]), "b_mod": f(inp["b_mod"]),
        "ln_g": f(inp["ln_g"]), "ln_b": f(inp["ln_b"]),
        "w_qkv": f(inp["attn_w_qkv"][0]), "lam": f(inp["attn_lambda"][0]),
        "subg": f(np.asarray(inp["attn_subln_g"])[0][None, :]),
        "w_o": f(inp["attn_w_o"][0]), "pool_w": f(inp["pool_w"][0]),
        "pool_scale": f(np.asarray(inp["pool_scale"])[0][None, :]),
        "w_in": f(inp["ffn_w_in"]), "w_out": f(inp["ffn_w_out"]),
        "rope_cos": cos128, "rope_sin": sin128, "band": band, "invedge": invedge,
    }


def kernel(**inputs):
    S = inputs["x"].shape[1]
    CTX = inputs["ctx"].shape[1]
    DFF = inputs["ffn_w_out"].shape[1]
    B = inputs["x"].shape[0]
    ncores = B // 2
    key = (S, CTX, DFF)
    if key not in _CACHE:
        _CACHE[key] = build(S, CTX, DFF)
    nc = _CACHE[key]
    in_maps = [make_core_inputs(inputs, c, S, CTX) for c in range(ncores)]
    res = run_bass_kernel_spmd(nc, in_maps, core_ids=list(range(ncores)))
    out = np.concatenate([np.asarray(r["out"]) for r in res.results], axis=0)
    return out.astype(np.float32)
```
